# Optimizing a Trainium2 kernel written in Bass

```python
import jax
import jax.numpy as jnp
from jax import lax
import numpy as np

D_MODEL = 2048
BATCH = 2
SEQ = 8192
DEPTH = 1

HEAD_DIM = 128
GRID_W = 64
NA_HEADS = 8
NA_KH = 8
NA_KW = 16
NA_QC = 16
NA_KC = NA_QC + NA_KW
DIL_GROUPS = ((128, 1), (512, 4), (2048, 16))
DIL_HEADS_PER_GROUP = 4
DIL_HEADS = DIL_HEADS_PER_GROUP * len(DIL_GROUPS)
DIL_BLOCK = 128
N_GROUPS = 8
EXPERTS_PER_GROUP = 8
N_EXPERTS = N_GROUPS * EXPERTS_PER_GROUP
TOP_K = 2
D_EXPERT = D_MODEL // 4
MOE_BLOCK = 128
LN_EPS = 1e-5
DN_ALPHA = (2 * DEPTH) ** 0.25
DN_BETA = (8 * DEPTH) ** -0.25
NA_WIDTH = NA_HEADS * HEAD_DIM
DIL_WIDTH = DIL_HEADS * HEAD_DIM
DIL_OUT_WIDTH = DIL_HEADS_PER_GROUP * HEAD_DIM
IN_WIDTHS = (NA_WIDTH, NA_WIDTH, NA_WIDTH, DIL_WIDTH, DIL_WIDTH, DIL_WIDTH, D_MODEL, D_MODEL)

kernel_name = 'hybrid_natten_dilated_hmoe_deepnorm'


def layer_norm(x, g, b):
    xf = x.astype(jnp.float32)
    mu = jnp.mean(xf, axis=-1, keepdims=True)
    var = jnp.mean(jnp.square(xf - mu), axis=-1, keepdims=True)
    y = (xf - mu) * lax.rsqrt(var + LN_EPS)
    return (y * g.astype(jnp.float32) + b.astype(jnp.float32)).astype(x.dtype)


def alibi_slopes(n):
    return np.array([2.0 ** (-8.0 * (i + 1) / n) for i in range(n)], dtype=np.float32)


def neighborhood_attention(q, k, v, rpb):
    B, H, S, hd = q.shape
    rows = S // GRID_W
    kh = min(NA_KH, rows)
    nc = GRID_W // NA_QC
    K = kh * NA_KC
    chunk_c0 = np.arange(nc) * NA_QC
    kc_start = np.clip(chunk_c0 - NA_KW // 2, 0, GRID_W - NA_KC)
    key_cols = kc_start[:, None] + np.arange(NA_KC)[None, :]
    qcol = chunk_c0[:, None] + np.arange(NA_QC)[None, :]
    qcs = np.clip(qcol - NA_KW // 2, 0, GRID_W - NA_KW)
    kc_slot = np.tile(key_cols, (1, kh))
    valid = jnp.asarray((kc_slot[:, None, :] >= qcs[:, :, None]) &
                        (kc_slot[:, None, :] < qcs[:, :, None] + NA_KW))
    dc = jnp.asarray(np.clip(kc_slot[:, None, :] - qcol[:, :, None] + NA_KW - 1,
                             0, 2 * NA_KW - 2), jnp.int32)
    ri_slot = jnp.asarray(np.repeat(np.arange(kh), NA_KC), jnp.int32)
    qg = q.reshape(B, H, rows, nc, NA_QC, hd)
    kg = k.reshape(B, H, rows, GRID_W, hd)
    vg = v.reshape(B, H, rows, GRID_W, hd)
    scale = HEAD_DIM ** -0.5

    def row_block(r):
        start = jnp.clip(r - kh // 2, 0, rows - kh)
        qr = lax.dynamic_index_in_dim(qg, r, axis=2, keepdims=False)
        kr = lax.dynamic_slice_in_dim(kg, start, kh, axis=2)[:, :, :, key_cols]
        vr = lax.dynamic_slice_in_dim(vg, start, kh, axis=2)[:, :, :, key_cols]
        kb = kr.transpose(0, 1, 3, 2, 4, 5).reshape(B, H, nc, K, hd)
        vb = vr.transpose(0, 1, 3, 2, 4, 5).reshape(B, H, nc, K, hd)
        dr = start - r + ri_slot + (NA_KH - 1)
        bias = rpb[:, dr[None, None, :], dc].astype(jnp.float32)
        s = jnp.einsum('bhcqe,bhcke->bhcqk', qr, kb,
                       preferred_element_type=jnp.float32) * scale + bias
        s = jnp.where(valid, s, -jnp.inf)
        p = jax.nn.softmax(s, axis=-1).astype(v.dtype)
        o = jnp.einsum('bhcqk,bhcke->bhcqe', p, vb)
        return o.reshape(B, H, GRID_W, hd)

    out = lax.map(row_block, jnp.arange(rows, dtype=jnp.int32))
    return out.transpose(1, 2, 0, 3, 4).reshape(B, H, S, hd)


def dilated_window_attention(q, k, v, slopes, dilation, half_w):
    B, G, S, hd = q.shape
    L = S // dilation
    blk = min(DIL_BLOCK, L)
    nb = -(-L // blk)
    Lp = nb * blk
    K = blk + 2 * half_w

    def split(t):
        return t.reshape(B, G, L, dilation, hd).transpose(0, 1, 3, 2, 4)

    qs = jnp.pad(split(q), ((0, 0), (0, 0), (0, 0), (0, Lp - L), (0, 0)))
    qs = qs.reshape(B, G, dilation, nb, blk, hd)
    pad_kv = ((0, 0), (0, 0), (0, 0), (half_w, Lp - L + half_w), (0, 0))
    kidx = np.arange(nb)[:, None] * blk + np.arange(K)[None, :]
    kb = jnp.pad(split(k), pad_kv)[:, :, :, kidx]
    vb = jnp.pad(split(v), pad_kv)[:, :, :, kidx]
    qpos = np.arange(nb)[:, None] * blk + np.arange(blk)[None, :]
    kpos = kidx - half_w
    rel = kpos[:, None, :] - qpos[:, :, None]
    in_seq = (kpos[:, None, :] >= 0) & (kpos[:, None, :] < L)
    valid = jnp.asarray((np.abs(rel) <= half_w) & (in_seq | (rel == 0)))
    dist = jnp.asarray(np.abs(rel) * dilation, jnp.float32)
    alibi = -jnp.asarray(slopes, jnp.float32)[:, None, None, None, None] * dist
    s = jnp.einsum('bgdnqe,bgdnke->bgdnqk', qs, kb,
                   preferred_element_type=jnp.float32) * HEAD_DIM ** -0.5 + alibi
    s = jnp.where(valid, s, -jnp.inf)
    lse = jax.nn.logsumexp(s, axis=-1)
    p = jnp.exp(s - lse[..., None]).astype(v.dtype)
    o = jnp.einsum('bgdnqk,bgdnke->bgdnqe', p, vb)
    o = o.reshape(B, G, dilation, Lp, hd)[:, :, :, :L].transpose(0, 1, 3, 2, 4).reshape(B, G, S, hd)
    lse = lse.reshape(B, G, dilation, Lp)[..., :L].transpose(0, 1, 3, 2).reshape(B, G, S)
    return o, lse


def dilated_mixture(q, k, v):
    slopes = alibi_slopes(DIL_HEADS)
    G = DIL_HEADS_PER_GROUP
    outs, lses = [], []
    for g, (window, dil) in enumerate(DIL_GROUPS):
        sl = slice(g * G, (g + 1) * G)
        o, lse = dilated_window_attention(q[:, sl], k[:, sl], v[:, sl], slopes[sl],
                                          dil, (window // 2) // dil)
        outs.append(o)
        lses.append(lse)
    w = jax.nn.softmax(jnp.stack(lses), axis=0)
    o = jnp.sum(w[..., None] * jnp.stack(outs).astype(jnp.float32), axis=0)
    return o.astype(q.dtype)


def token_mixer(h, w_in, b_in, rpb, w_proj_a, w_proj_b, w_o, b_o):
    B, S, D = h.shape
    z = jnp.einsum('bsd,de->bse', h, w_in) + b_in
    splits = np.cumsum(IN_WIDTHS)[:-1].tolist()
    qa, ka, va, qb, kb, vb, ga, gb = jnp.split(z, splits, axis=-1)

    def heads(t, n):
        return t.reshape(B, S, n, HEAD_DIM).transpose(0, 2, 1, 3)

    ya = neighborhood_attention(heads(qa, NA_HEADS), heads(ka, NA_HEADS), heads(va, NA_HEADS), rpb)
    ya = ya.transpose(0, 2, 1, 3).reshape(B, S, NA_WIDTH) @ w_proj_a
    yb = dilated_mixture(heads(qb, DIL_HEADS), heads(kb, DIL_HEADS), heads(vb, DIL_HEADS))
    yb = yb.transpose(0, 2, 1, 3).reshape(B, S, DIL_OUT_WIDTH) @ w_proj_b
    m = jax.nn.sigmoid(ga) * ya + jax.nn.sigmoid(gb) * yb
    return m @ w_o + b_o


def hierarchical_moe(h, w_rg, b_rg, w_re, b_re, w_gate, w_up, w_down):
    B, S, D = h.shape
    N = B * S
    xf = h.reshape(N, D)
    g_logits = (xf @ w_rg + b_rg).astype(jnp.float32)
    g_sel = jnp.argmax(g_logits, axis=-1).astype(jnp.int32)
    g_prob = jnp.take_along_axis(jax.nn.softmax(g_logits, axis=-1), g_sel[:, None], axis=-1)
    e_logits = (xf @ w_re + b_re).astype(jnp.float32).reshape(N, N_GROUPS, EXPERTS_PER_GROUP)
    e_logits = jnp.take_along_axis(e_logits, g_sel[:, None, None], axis=1)[:, 0]
    top_val, top_idx = lax.top_k(e_logits, TOP_K)
    gate = g_prob * jax.nn.softmax(top_val, axis=-1)
    expert_id = (g_sel[:, None] * EXPERTS_PER_GROUP + top_idx).astype(jnp.int32)

    A = N * TOP_K
    e_flat = expert_id.reshape(A)
    w_flat = gate.reshape(A).astype(h.dtype)
    tok = jnp.repeat(jnp.arange(N, dtype=jnp.int32), TOP_K)
    order = jnp.argsort(e_flat)
    e_s, tok_s, w_s = e_flat[order], tok[order], w_flat[order]
    counts = jnp.bincount(e_flat, length=N_EXPERTS).astype(jnp.int32)
    starts = jnp.cumsum(counts) - counts
    pcounts = (counts + MOE_BLOCK - 1) // MOE_BLOCK * MOE_BLOCK
    pends = jnp.cumsum(pcounts)
    pstarts = pends - pcounts
    dest = pstarts[e_s] + (jnp.arange(A, dtype=jnp.int32) - starts[e_s])
    nb = -(-A // MOE_BLOCK) + N_EXPERTS
    P = nb * MOE_BLOCK
    slot_tok = jnp.full((P,), N, jnp.int32).at[dest].set(tok_s)
    slot_w = jnp.zeros((P,), h.dtype).at[dest].set(w_s)
    block_e = jnp.minimum(jnp.searchsorted(pends, jnp.arange(nb, dtype=jnp.int32) * MOE_BLOCK,
                                           side='right'), N_EXPERTS - 1).astype(jnp.int32)
    xpad = jnp.concatenate([xf, jnp.zeros((1, D), xf.dtype)], axis=0)
    xs = xpad[slot_tok].reshape(nb, MOE_BLOCK, D)

    def expert_block(args):
        xb, e = args
        hid = jax.nn.silu(xb @ w_gate[e]) * (xb @ w_up[e])
        return hid @ w_down[e]

    ys = lax.map(expert_block, (xs, block_e)).reshape(P, D) * slot_w[:, None]
    y = jax.ops.segment_sum(ys, slot_tok, num_segments=N + 1)[:N]
    return y.reshape(B, S, D)


def setup_inputs(seed: int = 0) -> dict:
    key = jax.random.key(seed)
    ks = jax.random.split(key, 24)

    def nrm(k, shape, scale):
        return jax.random.normal(k, shape, jnp.float32) * scale

    col_scale = np.concatenate([np.full((w,), DN_BETA if i in (2, 5) else 1.0, np.float32)
                                for i, w in enumerate(IN_WIDTHS)])
    d_in = int(sum(IN_WIDTHS))
    return {
        'x': nrm(ks[0], (BATCH, SEQ, D_MODEL), 1.0),
        'ln0_g': 1.0 + nrm(ks[1], (D_MODEL,), 0.02),
        'ln0_b': nrm(ks[2], (D_MODEL,), 0.02),
        'w_in': nrm(ks[3], (DEPTH, D_MODEL, d_in), D_MODEL ** -0.5) * jnp.asarray(col_scale),
        'b_in': nrm(ks[4], (DEPTH, d_in), 0.02),
        'rpb': nrm(ks[5], (DEPTH, NA_HEADS, 2 * NA_KH - 1, 2 * NA_KW - 1), 0.1),
        'w_proj_a': nrm(ks[6], (DEPTH, NA_WIDTH, D_MODEL), NA_WIDTH ** -0.5 * DN_BETA),
        'w_proj_b': nrm(ks[7], (DEPTH, DIL_OUT_WIDTH, D_MODEL), DIL_OUT_WIDTH ** -0.5 * DN_BETA),
        'w_o': nrm(ks[8], (DEPTH, D_MODEL, D_MODEL), D_MODEL ** -0.5 * DN_BETA),
        'b_o': nrm(ks[9], (DEPTH, D_MODEL), 0.02),
        'ln1_g': 1.0 + nrm(ks[10], (DEPTH, D_MODEL), 0.02),
        'ln1_b': nrm(ks[11], (DEPTH, D_MODEL), 0.02),
        'w_router_group': nrm(ks[12], (DEPTH, D_MODEL, N_GROUPS), D_MODEL ** -0.5),
        'b_router_group': nrm(ks[13], (DEPTH, N_GROUPS), 0.01),
        'w_router_expert': nrm(ks[14], (DEPTH, D_MODEL, N_EXPERTS), D_MODEL ** -0.5),
        'b_router_expert': nrm(ks[15], (DEPTH, N_EXPERTS), 0.01),
        'w_gate': nrm(ks[16], (DEPTH, N_EXPERTS, D_MODEL, D_EXPERT), D_MODEL ** -0.5 * DN_BETA),
        'w_up': nrm(ks[17], (DEPTH, N_EXPERTS, D_MODEL, D_EXPERT), D_MODEL ** -0.5 * DN_BETA),
        'w_down': nrm(ks[18], (DEPTH, N_EXPERTS, D_EXPERT, D_MODEL), D_EXPERT ** -0.5 * DN_BETA),
        'ln2_g': 1.0 + nrm(ks[19], (DEPTH, D_MODEL), 0.02),
        'ln2_b': nrm(ks[20], (DEPTH, D_MODEL), 0.02),
    }


def reference(x, ln0_g, ln0_b, w_in, b_in, rpb, w_proj_a, w_proj_b, w_o, b_o, ln1_g, ln1_b,
              w_router_group, b_router_group, w_router_expert, b_router_expert,
              w_gate, w_up, w_down, ln2_g, ln2_b):
    h = layer_norm(x, ln0_g, ln0_b)
    for l in range(DEPTH):
        mix = token_mixer(h, w_in[l], b_in[l], rpb[l], w_proj_a[l], w_proj_b[l], w_o[l], b_o[l])
        h = layer_norm(DN_ALPHA * h + mix, ln1_g[l], ln1_b[l])
        ffn = hierarchical_moe(h, w_router_group[l], b_router_group[l], w_router_expert[l],
                               b_router_expert[l], w_gate[l], w_up[l], w_down[l])
        h = layer_norm(DN_ALPHA * h + ffn, ln2_g[l], ln2_b[l])
    return h
```

```python
from contextlib import ExitStack

import numpy as np

import concourse.bass as bass
import concourse.mybir as mybir
from concourse.bass_utils import run_bass_kernel_spmd

F32 = mybir.dt.float32
BF16 = mybir.dt.bfloat16
I32 = mybir.dt.int32
AF = mybir.ActivationFunctionType
ALU = mybir.AluOpType
AX = mybir.AxisListType

NCORES = 8
D = 2048
NCH = D // 128
TOK = 2048
HALO = 1024
NTILE = TOK // 128
D_IN = 11776
NA_H = 8
DIL = (1, 4, 16)
NEG = -30000.0
LN_EPS = 1e-5
ALPHA = 2.0 ** 0.25
SCALE = 128.0 ** -0.5
NEXP = 64
DE = 512
CAP = 128


class Buf:
    __slots__ = ("name", "w", "r", "sem", "dma_total", "multi", "wset")

    def __init__(self, name, multi=False):
        self.name = name
        self.w = None
        self.r = {}
        self.sem = None
        self.dma_total = 0
        self.multi = multi
        self.wset = {}


class T:
    def __init__(self, h, name):
        self.h = h
        self.buf = Buf(name)

    def __getitem__(self, k):
        return self.h[k]


class Prog:
    def __init__(self, nc):
        self.nc = nc
        self.e = {"pe": nc.tensor, "act": nc.scalar, "dve": nc.vector, "pool": nc.gpsimd, "sp": nc.sync}
        self.sem = {k: nc.alloc_semaphore("sem_" + k) for k in ("pe", "act", "dve", "pool")}
        self.cnt = {k: 0 for k in self.sem}
        self.waited = {k: {} for k in self.e}
        self.dbufs = []
        self.nsem = 0

    def _wait(self, eng, t):
        if t is None:
            return
        if t[0] == "c":
            key, val, sem = t[1], t[2], self.sem[t[1]]
            if eng == "pe" and key == "pe":
                return
        else:
            b = t[1]
            key, val, sem = ("d", id(b)), b.dma_total, b.sem
        if self.waited[eng].get(key, 0) >= val:
            return
        self.e[eng].wait_ge(sem, val)
        self.waited[eng][key] = val

    def _deps(self, eng, reads, writes, nw=False):
        for b in reads:
            if b.multi:
                for t in b.wset.values():
                    self._wait(eng, t)
            else:
                self._wait(eng, b.w)
        for b in writes:
            if not b.multi:
                if not (nw and b.w is not None and b.w[0] == "c" and b.w[1] == eng):
                    self._wait(eng, b.w)
            for t in b.r.values():
                self._wait(eng, t)

    def _mark(self, t, rkey, reads, writes):
        for b in reads:
            b.r[rkey] = t
        for b in writes:
            if b.multi:
                b.wset[rkey] = t
            else:
                b.w = t
            b.r = {}

    def op(self, eng, fn, reads=(), writes=(), nw=False):
        reads = [x.buf if isinstance(x, T) else x for x in reads]
        writes = [x.buf if isinstance(x, T) else x for x in writes]
        self._deps(eng, reads, writes, nw)
        ins = fn(self.e[eng])
        self.cnt[eng] += 1
        ins.then_inc(self.sem[eng], 1)
        self._mark(("c", eng, self.cnt[eng]), eng, reads, writes)

    def dma(self, q, out, in_, sb, reads=(), writes=(), indirect=None, **kw):
        reads = [x.buf if isinstance(x, T) else x for x in reads]
        writes = [x.buf if isinstance(x, T) else x for x in writes]
        sb = sb.buf if isinstance(sb, T) else sb
        self._deps(q, reads, writes)
        if sb.sem is None:
            sb.sem = self.nc.alloc_semaphore("dsem_%d" % self.nsem)
            self.nsem += 1
            self.dbufs.append(sb)
        if indirect is None:
            ins = self.e[q].dma_start(out=out, in_=in_, **kw)
        else:
            ins = self.e[q].indirect_dma_start(out=out, in_=in_, **indirect)
        ins.then_inc(sb.sem, 16)
        sb.dma_total += 16
        self._mark(("d", sb), ("d", id(sb)), reads, writes)

    def barrier(self):
        for eng in self.e:
            for k in self.sem:
                if self.cnt[k]:
                    self._wait(eng, ("c", k, self.cnt[k]))
            for b in self.dbufs:
                self._wait(eng, ("d", b))


class Cfg:
    def __init__(self, **kw):
        self.stage = 99
        self.debug = ()
        self.gather_w = False
        self.do_na = True
        self.do_dil = True
        self.nexp = NEXP
        self.__dict__.update(kw)


def build(cfg):
    nc = bass.Bass("TRN2", target_bir_lowering=False)
    P = Prog(nc)
    dbg = set(cfg.debug)

    def dram_in(name, shape, dt=F32):
        return nc.dram_tensor(name, list(shape), dt, kind="ExternalInput").ap()

    def scratch(name, shape, dt=BF16):
        kind = "ExternalOutput" if name in dbg else "Internal"
        h = nc.dram_tensor(name, list(shape), dt, kind=kind).ap()
        return h, Buf(name, multi=True)

    xh = dram_in("xh", [TOK + 2 * HALO, D])
    w_in = dram_in("w_in", [D, D_IN])
    cvec = dram_in("cvec", [128, 160])
    bv = dram_in("bv", [1, 2560])
    identf = dram_in("identf", [128, 128])
    natab = dram_in("natab", [NTAB * NA_H, 128, 128])
    abt = dram_in("abt", [12, 128, 256])
    kbias = dram_in("kbias", [128, 96])
    OTd, OTdb = scratch("OTd", [12, 128, TOK])
    wpa = dram_in("wpa", [1024, D])
    wpb = dram_in("wpb", [512, D])
    wo = dram_in("wo", [D, D])
    rowv = dram_in("rowv", [1, 7 * D])
    wr = dram_in("wr", [D, 72])
    rowr = dram_in("rowr", [1, 72 + 64])
    ustr = dram_in("ustr", [128, 128])
    ne = cfg.nexp
    w_gate = dram_in("w_gate", [ne, D, DE])
    w_up = dram_in("w_up", [ne, D, DE])
    w_down = dram_in("w_down", [ne, DE, D])
    MT, MTb = scratch("MT", [NCH, 128, TOK])
    H1, H1b_ = scratch("H1", [TOK, D], F32)
    Xs, Xsb = scratch("Xs", [NEXP * CAP, D])
    Ys, Ysb = scratch("Ys", [NEXP * CAP, D], F32)
    RT, RTb = scratch("RT", [TOK, 8], F32)
    NCVT = 32
    WgB, WgBb = scratch("WgB", [NCVT, D, DE])
    WuB, WuBb = scratch("WuB", [NCVT, D, DE])
    WdB, WdBb = scratch("WdB", [NCVT, DE, D])
    cvt_state = {"n": 0}

    def cvt_next():
        m = cvt_state["n"]
        if m >= 3 * NCVT:
            return
        cvt_state["n"] += 1
        e_ = m // 3
        dst, dstb, src, cc = ((WgB, WgBb, w_gate, NCH), (WuB, WuBb, w_up, NCH), (WdB, WdBb, w_down, 4))[m % 3]
        P.dma("pool", out=dst[e_].rearrange("(p c) n -> p (c n)", c=cc),
              in_=src[e_].rearrange("(p c) n -> p (c n)", c=cc), sb=dstb, writes=[dstb])

    out_d = nc.dram_tensor("out", [TOK, D], F32, kind="ExternalOutput").ap()
    out_b = Buf("out", multi=True)

    QA, QAb = scratch("QA", [NA_H, 128, TOK])
    KA, KAb = scratch("KA", [NA_H, 128, TOK + 512])
    VA, VAb = scratch("VA", [TOK + 512, 1024])
    QB, QBb, KB, KBb, VB, VBb = [], [], [], [], [], []
    for g, d in enumerate(DIL):
        a, b = scratch("QB%d" % g, [4, 128, TOK]); QB.append(a); QBb.append(b)
        a, b = scratch("KB%d" % g, [4, 128, d, TOK // d + 128]); KB.append(a); KBb.append(b)
        hal = 256 if g < 2 else 1024
        a, b = scratch("VB%d" % g, [TOK + 2 * hal, 512]); VB.append(a); VBb.append(b)
    GA, GAb = scratch("GA", [NCH, 128, TOK])
    GB, GBb = scratch("GB", [NCH, 128, TOK])

    outs = {}
    es = ExitStack()

    def sb(name, shape, dt, stack=es):
        return T(stack.enter_context(nc.sbuf_tensor(name, list(shape), dt)), name)

    def ps(name, shape, dt, stack=es):
        return T(stack.enter_context(nc.psum_tensor(name, list(shape), dt)), name)

    with es:
        cv = sb("cv", [128, 160], F32)
        P.dma("sp", out=cv[:], in_=cvec, sb=cv, writes=[cv])
        ident_b = sb("ident_b", [128, 128], BF16)
        P.dma("pool", out=ident_b[:], in_=identf, sb=ident_b, writes=[ident_b])
        g0T = cv[:, 0:16]
        b0T = cv[:, 16:32]
        binT = cv[:, 32:124]
        epsc = sb("epsc", [128, 1], F32)
        P.op("dve", lambda e: e.memset(epsc[:], LN_EPS), writes=[epsc])
        bq = sb("bq", [128, 92], F32)
        P.op("dve", lambda e: e.tensor_scalar(out=bq[:], in0=binT, scalar1=SCALE, scalar2=None, op0=ALU.mult),
             reads=[cv], writes=[bq])

        with ExitStack() as s1:
            hT = sb("hT", [128, NCH, TOK], BF16, s1)
            xt = [sb("xt%d" % i, [128, D], F32, s1) for i in range(2)]
            xn = [sb("xn%d" % i, [128, D], BF16, s1) for i in range(2)]
            stt = [sb("stt%d" % i, [128, 4, 6], F32, s1) for i in range(2)]
            mv = [sb("mv%d" % i, [128, 4], F32, s1) for i in range(2)]
            wb = [sb("wb%d" % i, [128, NCH, 512], BF16, s1) for i in range(2)]
            stg = [sb("stg%d" % i, [128, TOK], BF16, s1) for i in range(2)]
            vst = [sb("vst%d" % i, [128, 512], BF16, s1) for i in range(3)]
            bvb = sb("bvb", [128, 2560], F32, s1)
            P.dma("sp", out=bvb[:], in_=bv.partition_broadcast(128), sb=bvb, writes=[bvb])
            tp = [ps("tp%d" % i, [128, 8, 128], BF16, s1) for i in range(2)]
            acc = [ps("acc%d" % i, [128, 512], F32, s1) for i in range(4)]
            cnt = {"x": 0, "acc": 0, "stg": 0, "vst": 0, "wb": 0}

            def ln0_tiles(rows, slots):
                for r0, s0 in zip(rows, slots):
                    i = cnt["x"] % 2
                    cnt["x"] += 1
                    X, XN, ST, MV = xt[i], xn[i], stt[i], mv[i]
                    P.dma("sp", out=X[:], in_=xh[r0:r0 + 128, :], sb=X, writes=[X])
                    for c in range(4):
                        P.op("dve", lambda e, c=c: e.bn_stats(out=ST[:, c, :], in_=X[:, c * 512:(c + 1) * 512]),
                             reads=[X], writes=[ST])
                    P.op("dve", lambda e: e.bn_aggr(out=MV[:, 0:2], in_=ST[:]), reads=[ST], writes=[MV])
                    P.op("act", lambda e: e.activation(out=MV[:, 3:4], in_=MV[:, 1:2], func=AF.Sqrt, bias=epsc[:, 0:1],
                                                        scale=1.0), reads=[MV, epsc], writes=[MV])
                    P.op("dve", lambda e: e.reciprocal(out=MV[:, 2:3], in_=MV[:, 3:4]), reads=[MV], writes=[MV])
                    P.op("dve", lambda e: e.tensor_scalar(out=XN[:], in0=X[:], scalar1=MV[:, 0:1],
                                                           scalar2=MV[:, 2:3], op0=ALU.subtract, op1=ALU.mult),
                         reads=[X, MV], writes=[XN])
                    for half in range(2):
                        TP = tp[half]
                        for j in range(8):
                            c = half * 8 + j
                            P.op("pe", lambda e, c=c, j=j: e.transpose(out=TP[:, j, :], in_=XN[:, c * 128:(c + 1) * 128],
                                                                        identity=ident_b[:]),
                                 reads=[XN, ident_b], writes=[TP])
                        for j in range(8):
                            c = half * 8 + j
                            P.op("act", lambda e, c=c, j=j: e.activation(out=hT[:, c, s0:s0 + 128], in_=TP[:, j, :],
                                                                          func=AF.Identity, scale=g0T[:, c:c + 1],
                                                                          bias=b0T[:, c:c + 1]),
                                 reads=[TP, cv], writes=[hT], nw=True)

            def load_w(cb):
                W = wb[cnt["wb"] % 2]
                cnt["wb"] += 1
                src = w_in[:, cb * 512:(cb + 1) * 512].rearrange("(c p) n -> p c n", p=128)
                P.dma("pool", out=W[:], in_=src, sb=W, writes=[W])
                return W

            def fm_block(W, hh, groups, evac):
                for (s0, n) in groups:
                    A = acc[cnt["acc"] % 4]
                    cnt["acc"] += 1
                    for c in range(NCH):
                        P.op("pe", lambda e, c=c: e.matmul(A[:, 0:n], lhsT=W[:, c, hh * 128:(hh + 1) * 128],
                                                            rhs=hT[:, c, s0:s0 + n], start=(c == 0), stop=(c == NCH - 1)),
                             reads=[W, hT], writes=[A])
                    evac(A, s0, n)

            def tm_block(W, tiles, evac):
                for s0 in tiles:
                    A = acc[cnt["acc"] % 4]
                    cnt["acc"] += 1
                    for c in range(NCH):
                        P.op("pe", lambda e, c=c: e.matmul(A[:], lhsT=hT[:, c, s0:s0 + 128], rhs=W[:, c, :],
                                                            start=(c == 0), stop=(c == NCH - 1)),
                             reads=[W, hT], writes=[A])
                    evac(A, s0)

            def next_stg():
                S = stg[cnt["stg"] % 2]
                cnt["stg"] += 1
                return S

            def v_evac(dst, dstb, boff, rowmap):
                def f(A, s0):
                    V = vst[cnt["vst"] % 3]
                    cnt["vst"] += 1
                    P.op("dve", lambda e: e.tensor_tensor(out=V[:], in0=A[:], in1=bvb[:, boff:boff + 512], op=ALU.add),
                         reads=[A, bvb], writes=[V])
                    r = rowmap(s0)
                    P.dma("sp", out=dst[r:r + 128, :] if dst.shape[1] == 512 else dst[r:r + 128, boff:boff + 512],
                          in_=V[:], sb=V, reads=[V], writes=[dstb])
                return f

            ln0_tiles([HALO + i * 128 for i in range(NTILE)], [i * 128 for i in range(NTILE)])
            own_groups = [(tb * 512, 512) for tb in range(4)]
            own_tiles = [i * 128 for i in range(NTILE)]

            def passA_block(cb):
                W = load_w(cb)
                if cb in (0, 1, 2, 3):
                    for hh in range(4):
                        h = (cb % 2) * 4 + hh
                        col = cb * 4 + hh
                        S = next_stg()
                        if cb < 2:
                            ev = lambda A, s0, n, S=S, col=col: P.op(
                                "act", lambda e: e.activation(out=S[:, s0:s0 + n], in_=A[:, 0:n], func=AF.Identity,
                                                              scale=SCALE, bias=bq[:, col:col + 1]),
                                reads=[A, bq], writes=[S])
                        else:
                            ev = lambda A, s0, n, S=S, col=col: P.op(
                                "act", lambda e: e.activation(out=S[:, s0:s0 + n], in_=A[:, 0:n], func=AF.Identity,
                                                              scale=1.0, bias=binT[:, col:col + 1]),
                                reads=[A, cv], writes=[S])
                        fm_block(W, hh, own_groups, ev)
                        if cb < 2:
                            P.dma("sp", out=QA[h], in_=S[:], sb=S, reads=[S], writes=[QAb])
                        else:
                            P.dma("sp", out=KA[h][:, 256:256 + TOK], in_=S[:], sb=S, reads=[S], writes=[KAb])
                elif cb in (4, 5):
                    boff = (cb - 4) * 512
                    tm_block(W, own_tiles, v_evac(VA, VAb, boff, lambda s0: s0 + 256))
                elif 6 <= cb <= 11:
                    isq = cb < 9
                    g = (cb - 6) % 3
                    d = DIL[g]
                    for hh in range(4):
                        col = cb * 4 + hh
                        S = next_stg()
                        Sv = S[:].rearrange("p (r l) -> p r l", r=d)

                        def ev(A, s0, n, Sv=Sv, S=S, col=col, d=d, isq=isq):
                            l0 = s0 // d
                            src = A[:, 0:n].rearrange("p (l r) -> p r l", r=d)
                            P.op("act", lambda e: e.activation(
                                out=Sv[:, :, l0:l0 + n // d], in_=src, func=AF.Identity,
                                scale=(SCALE if isq else 1.0),
                                bias=(bq[:, col:col + 1] if isq else binT[:, col:col + 1])),
                                reads=[A, bq, cv], writes=[S])
                        fm_block(W, hh, own_groups, ev)
                        if isq:
                            P.dma("sp", out=QB[g][hh], in_=S[:], sb=S, reads=[S], writes=[QBb[g]])
                        else:
                            P.dma("sp", out=KB[g][hh][:, :, 64:64 + TOK // d], in_=Sv, sb=S, reads=[S], writes=[KBb[g]])
                elif 12 <= cb <= 14:
                    g = cb - 12
                    hal = 256 if g < 2 else 1024
                    boff = 1024 + g * 512
                    tm_block(W, own_tiles, v_evac(VB[g], VBb[g], boff, lambda s0, hal=hal: s0 + hal))
                else:
                    isa = cb < 19
                    for hh in range(4):
                        col = cb * 4 + hh
                        ch = (cb - (15 if isa else 19)) * 4 + hh
                        S = next_stg()
                        ev = lambda A, s0, n, S=S, col=col: P.op(
                            "act", lambda e: e.activation(out=S[:, s0:s0 + n], in_=A[:, 0:n], func=AF.Sigmoid,
                                                          scale=1.0, bias=binT[:, col:col + 1]),
                            reads=[A, cv], writes=[S])
                        fm_block(W, hh, own_groups, ev)
                        P.dma("sp", out=(GA if isa else GB)[ch], in_=S[:], sb=S, reads=[S],
                              writes=[GAb if isa else GBb])

            blocksA = cfg.blocksA if hasattr(cfg, "blocksA") else list(range(23))
            for cb in blocksA:
                passA_block(cb)
                cvt_next()

            if cfg.stage >= 1:
                rows = ([HALO - 256, HALO - 128] + [HALO + TOK, HALO + TOK + 128]
                        + [i * 128 for i in range(6)] + [HALO + TOK + 256 + i * 128 for i in range(6)])
                ln0_tiles(rows, [i * 128 for i in range(16)])

                def slot_tok(s0):
                    if s0 < 256:
                        return s0 - 256
                    if s0 < 512:
                        return TOK + (s0 - 256)
                    if s0 < 1280:
                        return -1024 + (s0 - 512)
                    return TOK + 256 + (s0 - 1280)

                near = [(0, 512)]
                allg = [(0, 512), (512, 512), (1024, 512), (1536, 512)]
                for cb in (2, 3, 4, 5, 9, 10, 11, 12, 13, 14):
                    W = load_w(cb)
                    cvt_next()
                    if cb in (2, 3):
                        for hh in range(4):
                            h = (cb % 2) * 4 + hh
                            col = cb * 4 + hh
                            S = next_stg()
                            ev = lambda A, s0, n, S=S, col=col: P.op(
                                "act", lambda e: e.activation(out=S[:, s0:s0 + n], in_=A[:, 0:n], func=AF.Identity,
                                                              scale=1.0, bias=binT[:, col:col + 1]),
                                reads=[A, cv], writes=[S])
                            fm_block(W, hh, near, ev)
                            P.dma("sp", out=KA[h][:, 0:256], in_=S[:, 0:256], sb=S, reads=[S], writes=[KAb])
                            P.dma("sp", out=KA[h][:, 256 + TOK:512 + TOK], in_=S[:, 256:512], sb=S, reads=[S], writes=[KAb])
                    elif cb in (4, 5):
                        boff = (cb - 4) * 512
                        tm_block(W, [0, 128, 256, 384], v_evac(VA, VAb, boff, lambda s0: slot_tok(s0) + 256))
                    elif 9 <= cb <= 11:
                        g = cb - 9
                        d = DIL[g]
                        for hh in range(4):
                            col = cb * 4 + hh
                            S = next_stg()
                            if g < 2:
                                nl = 256 // d
                                Sv = S[:, 0:512].rearrange("p (s r l) -> p s r l", s=2, r=d)

                                def ev(A, s0, n, Sv=Sv, S=S, col=col, d=d, nl=nl):
                                    src = A[:, 0:512].rearrange("p (s l r) -> p s r l", s=2, r=d)
                                    for s_ in range(2):
                                        P.op("act", lambda e, s_=s_: e.activation(
                                            out=Sv[:, s_, :, :], in_=src[:, s_, :, :], func=AF.Identity, scale=1.0,
                                            bias=binT[:, col:col + 1]), reads=[A, cv], writes=[S])
                                fm_block(W, hh, near, ev)
                                k = min(nl, 64)
                                P.dma("sp", out=KB[g][hh][:, :, 64 - k:64], in_=Sv[:, 0, :, nl - k:nl], sb=S,
                                      reads=[S], writes=[KBb[g]])
                                P.dma("sp", out=KB[g][hh][:, :, 64 + TOK // d:64 + TOK // d + k], in_=Sv[:, 1, :, 0:k],
                                      sb=S, reads=[S], writes=[KBb[g]])
                            else:
                                Sv = S[:].rearrange("p (s r l) -> p s r l", s=2, r=16)

                                def ev(A, s0, n, Sv=Sv, S=S, col=col):
                                    t0 = slot_tok(s0)
                                    side = 0 if t0 < 0 else 1
                                    off = (t0 + 1024) if side == 0 else (t0 - TOK)
                                    if s0 == 0:
                                        for s_, o in ((0, 768), (1, 0)):
                                            src = A[:, s_ * 256:(s_ + 1) * 256].rearrange("p (l r) -> p r l", r=16)
                                            P.op("act", lambda e, s_=s_, o=o, src=src: e.activation(
                                                out=Sv[:, s_, :, o // 16:o // 16 + 16], in_=src, func=AF.Identity,
                                                scale=1.0, bias=binT[:, col:col + 1]), reads=[A, cv], writes=[S])
                                        return
                                    pieces = []
                                    a = s0
                                    while a < s0 + n:
                                        bnd = 1280 if a < 1280 else 2048
                                        b = min(s0 + n, bnd)
                                        pieces.append((a, b))
                                        a = b
                                    for (a, b) in pieces:
                                        t0 = slot_tok(a)
                                        side = 0 if t0 < 0 else 1
                                        off = (t0 + 1024) if side == 0 else (t0 - TOK)
                                        src = A[:, a - s0:b - s0].rearrange("p (l r) -> p r l", r=16)
                                        P.op("act", lambda e, side=side, off=off, src=src, a=a, b=b: e.activation(
                                            out=Sv[:, side, :, off // 16:off // 16 + (b - a) // 16], in_=src,
                                            func=AF.Identity, scale=1.0, bias=binT[:, col:col + 1]),
                                            reads=[A, cv], writes=[S])
                                fm_block(W, hh, allg, ev)
                                P.dma("sp", out=KB[g][hh][:, :, 0:64], in_=Sv[:, 0], sb=S, reads=[S], writes=[KBb[g]])
                                P.dma("sp", out=KB[g][hh][:, :, 192:256], in_=Sv[:, 1], sb=S, reads=[S], writes=[KBb[g]])
                    else:
                        g = cb - 12
                        hal = 256 if g < 2 else 1024
                        boff = 1024 + g * 512
                        tiles = [0, 128, 256, 384] if g < 2 else [i * 128 for i in range(16)]
                        tm_block(W, tiles, v_evac(VB[g], VBb[g], boff, lambda s0, hal=hal: slot_tok(s0) + hal))
            P.barrier()

        zt = sb("zt", [128, D], BF16)
        P.op("pool", lambda e: e.memset(zt[:], 0.0), writes=[zt])
        for e_ in range(NEXP):
            P.dma("sp", out=Xs[e_ * CAP:(e_ + 1) * CAP, :], in_=zt[:], sb=zt, reads=[zt], writes=[Xsb])
        ident_f = sb("ident_f", [128, 128], F32)
        P.dma("sp", out=ident_f[:], in_=identf, sb=ident_f, writes=[ident_f])
        ones_b = sb("ones_b", [128, 128], BF16)
        P.op("dve", lambda e: e.memset(ones_b[:], 1.0), writes=[ones_b])
        dsti = sb("dsti", [128, 32], I32)
        gts = sb("gts", [128, 32], F32)
        if cfg.stage >= 2:
            with ExitStack() as sA:
                OT = sb("OT", [128, 12, TOK], BF16, sA)
                phase2(nc, P, cfg, sb, ps, dict(QA=QA, QAb=QAb, KA=KA, KAb=KAb, VA=VA, VAb=VAb, QB=QB, QBb=QBb, KB=KB,
                                                KBb=KBb, VB=VB, VBb=VBb, natab=natab, abt=abt, kbias=kbias,
                                                ident_b=ident_b, ones_b=ones_b, OT=OT, cvt=cvt_next))
                if "OTd" in dbg:
                    for h in range(12):
                        P.dma("sp", out=OTd[h], in_=OT[:, h, :], sb=OT, reads=[OT], writes=[OTdb])
                if cfg.stage >= 3:
                    phase3a(nc, P, cfg, sb, ps, dict(OT=OT, wpa=wpa, wpb=wpb, GA=GA, GAb=GAb, GB=GB, GBb=GBb, MT=MT, MTb=MTb, cvt=cvt_next))
                P.barrier()
        if cfg.stage >= 4:
            phase3b(nc, P, cfg, sb, ps, dict(MT=MT, MTb=MTb, wo=wo, rowv=rowv, wr=wr, rowr=rowr, ustr=ustr, xh=xh,
                                             ident_f=ident_f, ones_b=ones_b, H1=H1, H1b=H1b_, Xs=Xs, Xsb=Xsb, zt=zt, cvt=cvt_next,
                                             dsti=dsti, gts=gts, RT=RT, RTb=RTb))
            P.barrier()
        if cfg.stage >= 5:
            while cvt_state["n"] < 3 * NCVT:
                cvt_next()
            phase4(nc, P, cfg, sb, ps, dict(w_gate=w_gate, w_up=w_up, w_down=w_down, Xs=Xs, Xsb=Xsb, Ys=Ys, Ysb=Ysb,
                                            ident_b=ident_b, WgB=WgB, WgBb=WgBb, WuB=WuB, WuBb=WuBb, WdB=WdB, WdBb=WdBb,
                                            ncvt=NCVT))
            P.barrier()
        if cfg.stage >= 6:
            phase5(nc, P, cfg, sb, ps, dict(Ys=Ys, Ysb=Ysb, H1=H1, H1b=H1b_, rowv=rowv, dsti=dsti, gts=gts,
                                            out=out_d, outb=out_b))
            P.barrier()
        P.barrier()
    return nc


NA_CLASSES = {
    "gen": (0, (-2, -1, 0, 1, 2)),
    "p0": (5, (-2, -1, 0, 1, 2, 3)),
    "p1": (11, (-2, -1, 0, 1, 2)),
    "p14": (16, (-2, -1, 0, 1, 2)),
    "p15": (21, (-3, -2, -1, 0, 1, 2)),
}
NTAB = 27


def na_class(rp):
    return {0: "p0", 1: "p1", 14: "p14", 15: "p15"}.get(rp, "gen")


class Item:
    __slots__ = ("S", "E", "V", "N", "post")

    def __init__(self):
        self.post = None


def phase2(nc, P, cfg, sb, ps, a):
    OT = a["OT"]
    ident_b, ones_b = a["ident_b"], a["ones_b"]
    with ExitStack() as s2:
        nat = sb("nat", [128, NTAB * NA_H, 128], BF16, s2)
        half = NTAB * NA_H // 2
        for i in range(2):
            P.dma("pool", out=nat[:, i * half:(i + 1) * half, :],
                  in_=a["natab"][i * half:(i + 1) * half].rearrange("t p q -> p t q"), sb=nat, writes=[nat])
        abt = sb("abt_s", [128, 12, 256], BF16, s2)
        P.dma("pool", out=abt[:], in_=a["abt"].rearrange("h p q -> p h q"), sb=abt, writes=[abt])
        kbs = sb("kbs", [128, 96], F32, s2)
        P.dma("sp", out=kbs[:], in_=a["kbias"], sb=kbs, writes=[kbs])
        qT = [sb("qT%d" % i, [128, TOK], BF16, s2) for i in range(2)]
        kT = [sb("kT%d" % i, [128, 4096], BF16, s2) for i in range(2)]
        vt = [sb("vt%d" % i, [128, 32, 128], BF16, s2) for i in range(2)]
        PT = [sb("PT%d" % i, [128, 768], BF16, s2) for i in range(3)]
        rD = [sb("rD%d" % i, [128, 128], F32, s2) for i in range(2)]
        Uacc = sb("Uacc", [128, TOK], F32, s2)
        Dacc = sb("Dacc", [128, TOK], F32, s2)
        SA = [ps("SA%d" % i, [128, 1024], F32, s2) for i in range(2)]
        OB = [ps("OB%d" % i, [128, 512], F32, s2) for i in range(3)]
        groups = []
        kk = [0]

        def nextk():
            k = kk[0]
            kk[0] += 1
            return k

        for h in (range(NA_H) if cfg.do_na else ()):
            gi = len(groups)
            Q, Kt, V = qT[gi % 2], kT[gi % 2], vt[gi % 2]

            def load(h=h, Q=Q, Kt=Kt, V=V):
                P.dma("sp", out=Q[:], in_=a["QA"][h], sb=Q, reads=[a["QAb"]], writes=[Q])
                P.dma("sp", out=Kt[:, 0:TOK + 512], in_=a["KA"][h], sb=Kt, reads=[a["KAb"]], writes=[Kt])
                P.dma("sp", out=V[:, 0:20, :], in_=a["VA"][:, h * 128:(h + 1) * 128].rearrange("(t p) c -> p t c", p=128),
                      sb=V, reads=[a["VAb"]], writes=[V])
            items = []
            for rp in range(16):
                base, offs = NA_CLASSES[na_class(rp)]
                nt = len(offs)
                k = nextk()
                S, Pt, O, R = SA[k % 2], PT[k % 3], OB[k % 3], rD[k % 2]
                qs = slice(rp * 128, (rp + 1) * 128)
                it = Item()

                def fS(S=S, Q=Q, Kt=Kt, offs=offs, rp=rp, base=base, h=h, qs=qs):
                    for i, o in enumerate(offs):
                        kp = rp + o + 2
                        cs = slice(i * 128, (i + 1) * 128)
                        P.op("pe", lambda e: e.matmul(S[:, cs], lhsT=Kt[:, kp * 128:(kp + 1) * 128], rhs=Q[:, qs],
                                                      start=True, stop=False), reads=[Kt, Q], writes=[S])
                        P.op("pe", lambda e: e.matmul(S[:, cs], lhsT=ident_b[:], rhs=nat[:, (base + i) * NA_H + h, :],
                                                      start=False, stop=True), reads=[ident_b, nat], writes=[S])

                def fE(S=S, Pt=Pt, nt=nt):
                    n1 = min(nt, 4) * 128
                    P.op("act", lambda e: e.activation(out=Pt[:, 0:n1], in_=S[:, 0:n1], func=AF.Exp), reads=[S], writes=[Pt])
                    if nt > 4:
                        P.op("act", lambda e: e.activation(out=Pt[:, 512:nt * 128], in_=S[:, 512:nt * 128], func=AF.Exp),
                             reads=[S], writes=[Pt])

                def fV(Pt=Pt, O=O, V=V, offs=offs, rp=rp, nt=nt):
                    for i, o in enumerate(offs):
                        kp = rp + o + 2
                        P.op("pe", lambda e: e.matmul(O[:, 0:128], lhsT=V[:, kp, :], rhs=Pt[:, i * 128:(i + 1) * 128],
                                                      start=(i == 0), stop=(i == nt - 1)), reads=[V, Pt], writes=[O])
                    for i in range(nt):
                        P.op("pe", lambda e: e.matmul(O[:, 128:256], lhsT=ones_b[:], rhs=Pt[:, i * 128:(i + 1) * 128],
                                                      start=(i == 0), stop=(i == nt - 1)), reads=[ones_b, Pt], writes=[O])

                def fN(O=O, R=R, h=h, qs=qs):
                    P.op("dve", lambda e: e.reciprocal(out=R[:], in_=O[:, 128:256]), reads=[O], writes=[R])
                    P.op("dve", lambda e: e.tensor_tensor(out=OT[:, h, qs], in0=O[:, 0:128], in1=R[:], op=ALU.mult),
                         reads=[O, R], writes=[OT])
                it.S, it.E, it.V, it.N = fS, fE, fV, fN
                items.append(it)
            groups.append((load, items))
        for j in (range(4) if cfg.do_dil else ()):
            for g, d in enumerate(DIL):
                hd = 4 * g + j
                Lq = TOK // d
                Lk = Lq + 128
                nb = Lq // 128
                nm = nb + 1
                gi = len(groups)
                Q, Kt, V = qT[gi % 2], kT[gi % 2], vt[gi % 2]

                def load(g=g, j=j, d=d, Lk=Lk, nm=nm, Q=Q, Kt=Kt, V=V):
                    P.dma("sp", out=Q[:], in_=a["QB"][g][j], sb=Q, reads=[a["QBb"][g]], writes=[Q])
                    P.dma("sp", out=Kt[:, 0:d * Lk], in_=a["KB"][g][j].rearrange("p r l -> p (r l)"), sb=Kt,
                          reads=[a["KBb"][g]], writes=[Kt])
                    vbase = 192 if g == 0 else 0
                    for r in range(d):
                        src = a["VB"][g][vbase + r:vbase + r + (nm * 128 - 1) * d + 1:d, j * 128:(j + 1) * 128]
                        P.dma("sp", out=V[:, r * nm:(r + 1) * nm, :], in_=src.rearrange("(m p) c -> p m c", p=128),
                              sb=V, reads=[a["VBb"][g]], writes=[V])
                Ua = Uacc[:].rearrange("p (l r) -> p r l", r=d)
                Da = Dacc[:].rearrange("p (l r) -> p r l", r=d)
                items = []
                for r in range(d):
                    for n in range(nb):
                        k = nextk()
                        S, Pt, O = SA[k % 2], PT[k % 3], OB[k % 3]
                        qs = slice(r * Lq + n * 128, r * Lq + (n + 1) * 128)
                        it = Item()

                        def fS(S=S, Q=Q, Kt=Kt, r=r, n=n, Lk=Lk, qs=qs, hd=hd):
                            for t in range(2):
                                m = n + t
                                cs = slice(t * 128, (t + 1) * 128)
                                P.op("pe", lambda e: e.matmul(S[:, cs], lhsT=Kt[:, r * Lk + m * 128:r * Lk + (m + 1) * 128],
                                                              rhs=Q[:, qs], start=True, stop=False), reads=[Kt, Q], writes=[S])
                                P.op("pe", lambda e: e.matmul(S[:, cs], lhsT=ident_b[:],
                                                              rhs=abt[:, hd, (1 - t) * 128:(2 - t) * 128],
                                                              start=False, stop=True), reads=[ident_b, abt], writes=[S])

                        def fE(S=S, Pt=Pt, n=n, nb=nb, g=g, r=r):
                            lo_edge = (n == 0)
                            hi_edge = (n == nb - 1)
                            if not (lo_edge or hi_edge):
                                P.op("act", lambda e: e.activation(out=Pt[:, 0:256], in_=S[:, 0:256], func=AF.Exp),
                                     reads=[S], writes=[Pt])
                                return
                            for t in range(2):
                                edge = (t == 0 and lo_edge) or (t == 1 and hi_edge)
                                cs = slice(t * 128, (t + 1) * 128)
                                if edge:
                                    col = KB_COL(g, t, r)
                                    P.op("act", lambda e: e.activation(out=Pt[:, cs], in_=S[:, cs], func=AF.Exp,
                                                                       bias=kbs[:, col:col + 1], scale=1.0),
                                         reads=[S, kbs], writes=[Pt])
                                else:
                                    P.op("act", lambda e: e.activation(out=Pt[:, cs], in_=S[:, cs], func=AF.Exp),
                                         reads=[S], writes=[Pt])

                        def fV(Pt=Pt, O=O, V=V, r=r, n=n, nm=nm):
                            for t in range(2):
                                m = n + t
                                P.op("pe", lambda e: e.matmul(O[:, 0:128], lhsT=V[:, r * nm + m, :],
                                                              rhs=Pt[:, t * 128:(t + 1) * 128], start=(t == 0), stop=(t == 1)),
                                     reads=[V, Pt], writes=[O])
                            for t in range(2):
                                P.op("pe", lambda e: e.matmul(O[:, 128:256], lhsT=ones_b[:], rhs=Pt[:, t * 128:(t + 1) * 128],
                                                              start=(t == 0), stop=(t == 1)), reads=[ones_b, Pt], writes=[O])

                        def fN(O=O, Ua=Ua, Da=Da, r=r, n=n, g=g):
                            ls = slice(n * 128, (n + 1) * 128)
                            if g == 0:
                                P.op("dve", lambda e: e.tensor_copy(out=Ua[:, r, ls], in_=O[:, 0:128]), reads=[O], writes=[Uacc])
                                P.op("dve", lambda e: e.tensor_copy(out=Da[:, r, ls], in_=O[:, 128:256]), reads=[O], writes=[Dacc])
                            else:
                                P.op("dve", lambda e: e.tensor_tensor(out=Ua[:, r, ls], in0=O[:, 0:128], in1=Ua[:, r, ls],
                                                                      op=ALU.add), reads=[O, Uacc], writes=[Uacc])
                                P.op("dve", lambda e: e.tensor_tensor(out=Da[:, r, ls], in0=O[:, 128:256], in1=Da[:, r, ls],
                                                                      op=ALU.add), reads=[O, Dacc], writes=[Dacc])
                        it.S, it.E, it.V, it.N = fS, fE, fV, fN
                        items.append(it)
                if g == 2:
                    def post(j=j):
                        P.op("dve", lambda e: e.reciprocal(out=Dacc[:], in_=Dacc[:]), reads=[Dacc], writes=[Dacc])
                        P.op("dve", lambda e: e.tensor_tensor(out=OT[:, 8 + j, :], in0=Uacc[:], in1=Dacc[:], op=ALU.mult),
                             reads=[Uacc, Dacc], writes=[OT])
                    items[-1].post = post
                groups.append((load, items))
        flat = [(gi, ii, it) for gi, (ld, its) in enumerate(groups) for ii, it in enumerate(its)]
        if groups:
            groups[0][0]()
        prev = None
        for (gi, ii, it) in flat:
            it.S()
            if prev is not None:
                prev.E(); prev.V(); prev.N()
                if prev.post:
                    prev.post()
            if ii == 0 and gi + 1 < len(groups):
                groups[gi + 1][0]()
                a["cvt"]()
            prev = it
        if prev is not None:
            prev.E(); prev.V(); prev.N()
            if prev.post:
                prev.post()
        P.barrier()


def phase3a(nc, P, cfg, sb, ps, a):
    OT = a["OT"]
    with ExitStack() as st:
        Wa = sb("Wa", [128, 8, D], BF16, st)
        Wb = sb("Wb", [128, 4, D], BF16, st)
        for i in range(2):
            P.dma("pool", out=Wa[:, i * 4:(i + 1) * 4, :],
                  in_=a["wpa"][i * 512:(i + 1) * 512, :].rearrange("(c p) n -> p c n", p=128), sb=Wa, writes=[Wa])
        P.dma("pool", out=Wb[:], in_=a["wpb"].rearrange("(c p) n -> p c n", p=128), sb=Wb, writes=[Wb])
        gat = [sb("gat%d" % i, [128, 512], BF16, st) for i in range(2)]
        gbt = [sb("gbt%d" % i, [128, 512], BF16, st) for i in range(2)]
        t1 = [sb("t1_%d" % i, [128, 512], F32, st) for i in range(2)]
        t2 = [sb("t2_%d" % i, [128, 512], F32, st) for i in range(2)]
        ms = [sb("ms%d" % i, [128, TOK], BF16, st) for i in range(2)]
        ya = [ps("ya%d" % i, [128, 512], F32, st) for i in range(2)]
        yb = [ps("yb%d" % i, [128, 512], F32, st) for i in range(2)]
        k = 0
        for c in range(NCH):
            M = ms[c % 2]
            a["cvt"]()
            for tb in range(4):
                ts_ = slice(tb * 512, (tb + 1) * 512)
                YA, YB, G1, G2, T1, T2 = ya[k % 2], yb[k % 2], gat[k % 2], gbt[k % 2], t1[k % 2], t2[k % 2]
                k += 1
                P.dma("sp", out=G1[:], in_=a["GA"][c][:, ts_], sb=G1, reads=[a["GAb"]], writes=[G1])
                P.dma("sp", out=G2[:], in_=a["GB"][c][:, ts_], sb=G2, reads=[a["GBb"]], writes=[G2])
                for h in range(8):
                    P.op("pe", lambda e: e.matmul(YA[:], lhsT=Wa[:, h, c * 128:(c + 1) * 128], rhs=OT[:, h, ts_],
                                                  start=(h == 0), stop=(h == 7)), reads=[Wa, OT], writes=[YA])
                for j in range(4):
                    P.op("pe", lambda e: e.matmul(YB[:], lhsT=Wb[:, j, c * 128:(c + 1) * 128], rhs=OT[:, 8 + j, ts_],
                                                  start=(j == 0), stop=(j == 3)), reads=[Wb, OT], writes=[YB])
                P.op("dve", lambda e: e.tensor_tensor(out=T1[:], in0=YA[:], in1=G1[:], op=ALU.mult), reads=[YA, G1], writes=[T1])
                P.op("dve", lambda e: e.tensor_tensor(out=T2[:], in0=YB[:], in1=G2[:], op=ALU.mult), reads=[YB, G2], writes=[T2])
                P.op("pool", lambda e: e.tensor_tensor(out=M[:, ts_], in0=T1[:], in1=T2[:], op=ALU.add), reads=[T1, T2], writes=[M])
            P.dma("sp", out=a["MT"][c], in_=M[:], sb=M, reads=[M], writes=[a["MTb"]])


def phase3b(nc, P, cfg, sb, ps, a):
    dsti, gts = a["dsti"], a["gts"]
    ident_f, ones_b = a["ident_f"], a["ones_b"]
    with ExitStack() as st:
        Wo = sb("Wo", [128, NCH, D], BF16, st)
        for i in range(4):
            P.dma("pool", out=Wo[:, i * 4:(i + 1) * 4, :],
                  in_=a["wo"][i * 512:(i + 1) * 512, :].rearrange("(c p) n -> p c n", p=128), sb=Wo, writes=[Wo])
        rv = a["rowv"]
        A0 = sb("A0", [128, D], F32, st)
        B0 = sb("B0", [128, D], F32, st)
        G1v = sb("G1v", [128, D], F32, st)
        B1v = sb("B1v", [128, D], F32, st)
        tmpv = sb("tmpv", [128, D], F32, st)
        P.dma("sp", out=A0[:], in_=rv[:, 0:D].partition_broadcast(128), sb=A0, writes=[A0])
        P.dma("sp", out=B0[:], in_=rv[:, D:2 * D].partition_broadcast(128), sb=B0, writes=[B0])
        P.dma("sp", out=tmpv[:], in_=rv[:, 2 * D:3 * D].partition_broadcast(128), sb=tmpv, writes=[tmpv])
        P.dma("sp", out=G1v[:], in_=rv[:, 3 * D:4 * D].partition_broadcast(128), sb=G1v, writes=[G1v])
        P.dma("sp", out=B1v[:], in_=rv[:, 4 * D:5 * D].partition_broadcast(128), sb=B1v, writes=[B1v])
        P.op("dve", lambda e: e.tensor_scalar(out=A0[:], in0=A0[:], scalar1=ALPHA, scalar2=None, op0=ALU.mult),
             reads=[A0], writes=[A0])
        P.op("dve", lambda e: e.scalar_tensor_tensor(out=B0[:], in0=B0[:], scalar=ALPHA, in1=tmpv[:], op0=ALU.mult,
                                                     op1=ALU.add), reads=[B0, tmpv], writes=[B0])
        wrs = sb("wrs", [128, NCH, 72], F32, st)
        P.dma("sp", out=wrs[:], in_=a["wr"].rearrange("(c p) n -> p c n", p=128), sb=wrs, writes=[wrs])
        rr = sb("rr", [128, 136], F32, st)
        P.dma("sp", out=rr[:], in_=a["rowr"].partition_broadcast(128), sb=rr, writes=[rr])
        us = sb("us", [128, 128], BF16, st)
        P.dma("pool", out=us[:], in_=a["ustr"], sb=us, writes=[us])
        cntv = sb("cntv", [128, 64], F32, st)
        P.op("dve", lambda e: e.memset(cntv[:], 0.0), writes=[cntv])
        epsc = sb("epsc2", [128, 1], F32, st)
        P.op("dve", lambda e: e.memset(epsc[:], LN_EPS), writes=[epsc])
        zt = a["zt"]
        P.op("pool", lambda e: e.memset(zt[:, 0:8], 0.0), writes=[zt])

        mt = [sb("mt%d" % i, [128, NCH, 128], BF16, st) for i in range(2)]
        xt = [sb("x3_%d" % i, [128, D], F32, st) for i in range(2)]
        xn = sb("xn3", [128, D], F32, st)
        rt = sb("rt3", [128, D], F32, st)
        h1 = [sb("h1_%d" % i, [128, D], F32, st) for i in range(2)]
        h1b = [sb("h1b%d" % i, [128, D], BF16, st) for i in range(3)]
        h1T = sb("h1T", [128, NCH, 128], F32, st)
        stt = sb("stt3", [128, 4, 6], F32, st)
        mv = sb("mv3", [128, 8], F32, st)
        sm2 = [sb("sm3_%d" % i, [128, 512], F32, st) for i in range(2)]
        eb2 = [sb("eb3_%d" % i, [128, 64], BF16, st) for i in range(2)]
        mix = [ps("mix%d" % i, [128, 512], F32, st) for i in range(4)]
        tpf = [ps("tpf%d" % i, [128, 4, 128], F32, st) for i in range(2)]
        lgp = ps("lgp", [128, 128], F32, st)
        rkp = ps("rkp", [128, 128], F32, st)
        ntile = cfg.ntile3 if hasattr(cfg, "ntile3") else NTILE

        def ln_stats(X, col):
            for c in range(4):
                P.op("dve", lambda e, c=c: e.bn_stats(out=stt[:, c, :], in_=X[:, c * 512:(c + 1) * 512]), reads=[X], writes=[stt])
            P.op("dve", lambda e: e.bn_aggr(out=mv[:, col:col + 2], in_=stt[:]), reads=[stt], writes=[mv])
            P.op("act", lambda e: e.activation(out=mv[:, col + 3:col + 4], in_=mv[:, col + 1:col + 2], func=AF.Sqrt,
                                               bias=epsc[:, 0:1], scale=1.0), reads=[mv, epsc], writes=[mv])
            P.op("dve", lambda e: e.reciprocal(out=mv[:, col + 1:col + 2], in_=mv[:, col + 3:col + 4]), reads=[mv], writes=[mv])
            P.op("dve", lambda e: e.scalar_tensor_tensor(out=mv[:, col + 2:col + 3], in0=mv[:, col:col + 1], scalar=-1.0,
                                                         in1=mv[:, col + 1:col + 2], op0=ALU.mult, op1=ALU.mult),
                 reads=[mv], writes=[mv])

        def tile_vars(tt):
            return mt[tt % 2], xt[tt % 2], h1[tt % 2], h1b[tt % 3], slice(tt * 128, (tt + 1) * 128)

        def load(tt):
            M, X, H, HB, tsl = tile_vars(tt)
            P.dma("sp", out=M[:], in_=a["MT"][:, :, tsl].rearrange("c p t -> p c t"), sb=M, reads=[a["MTb"]], writes=[M])
            P.dma("sp", out=X[:], in_=a["xh"][HALO + tt * 128:HALO + (tt + 1) * 128, :], sb=X, writes=[X])

        def stageA(tt, part):
            M, X, H, HB, tsl = tile_vars(tt)
            if part == 'pe':
                for nb in range(4):
                    for c in range(NCH):
                        P.op("pe", lambda e: e.matmul(mix[nb][:], lhsT=M[:, c, :], rhs=Wo[:, c, nb * 512:(nb + 1) * 512],
                                                      start=(c == 0), stop=(c == NCH - 1)), reads=[M, Wo], writes=[mix[nb]])
                return
            if part == 'res':
                ln_stats(X, 0)
                P.op("act", lambda e: e.activation(out=xn[:], in_=X[:], func=AF.Identity, scale=mv[:, 1:2], bias=mv[:, 2:3]),
                     reads=[X, mv], writes=[xn])
                P.op("pool", lambda e: e.tensor_tensor(out=xn[:], in0=xn[:], in1=A0[:], op=ALU.mult), reads=[xn, A0], writes=[xn])
                P.op("pool", lambda e: e.tensor_tensor(out=xn[:], in0=xn[:], in1=B0[:], op=ALU.add), reads=[xn, B0], writes=[xn])
                for nb in range(4):
                    cs = slice(nb * 512, (nb + 1) * 512)
                    P.op("dve", lambda e: e.tensor_tensor(out=rt[:, cs], in0=mix[nb][:], in1=xn[:, cs], op=ALU.add),
                         reads=[mix[nb], xn], writes=[rt])
                return
            if part == 'ln1':
                ln_stats(rt, 4)
                P.op("act", lambda e: e.activation(out=H[:], in_=rt[:], func=AF.Identity, scale=mv[:, 5:6], bias=mv[:, 6:7]),
                     reads=[rt, mv], writes=[H])
                P.op("pool", lambda e: e.tensor_tensor(out=H[:], in0=H[:], in1=G1v[:], op=ALU.mult), reads=[H, G1v], writes=[H])
                return
            P.op("dve", lambda e: e.tensor_tensor(out=H[:], in0=H[:], in1=B1v[:], op=ALU.add), reads=[H, B1v], writes=[H])
            P.dma("sp", out=a["H1"][tsl, :], in_=H[:], sb=H, reads=[H], writes=[a["H1b"]])
            P.op("act", lambda e: e.copy(out=HB[:].rearrange("t (c p) -> t c p", p=128),
                                         in_=H[:].rearrange("t (p c) -> t c p", c=NCH)), reads=[H], writes=[HB])

        def stageB(tt, part):
            M, X, H, HB, tsl = tile_vars(tt)
            sm = sm2[tt % 2]
            eb = eb2[tt % 2]
            if part == '1pe':
                for q4 in range(4):
                    TP = tpf[q4 % 2]
                    for j in range(4):
                        c = q4 * 4 + j
                        P.op("pe", lambda e: e.transpose(out=TP[:, j, :], in_=H[:, c * 128:(c + 1) * 128], identity=ident_f[:]),
                             reads=[H, ident_f], writes=[TP])
                    P.op("act", lambda e: e.copy(out=h1T[:, q4 * 4:(q4 + 1) * 4, :], in_=TP[:]), reads=[TP], writes=[h1T], nw=True)
                for c in range(NCH):
                    P.op("pe", lambda e: e.matmul(lgp[:, 0:72], lhsT=h1T[:, c, :], rhs=wrs[:, c, :], start=(c == 0),
                                                  stop=(c == NCH - 1)), reads=[h1T, wrs], writes=[lgp])
            V = lambda fn, r, w: P.op("dve", fn, reads=r, writes=w)
            Lg = sm[:, 0:72]
            if part == '1dve':
                V(lambda e: e.tensor_tensor(out=Lg, in0=lgp[:, 0:72], in1=rr[:, 0:72], op=ALU.add), [lgp, rr], [sm])
            gl = sm[:, 0:8]
            el3 = sm[:, 8:72].rearrange("p (g j) -> p g j", g=8)
            gmax, ngmax, gsum, gprob = sm[:, 80:81], sm[:, 81:82], sm[:, 82:83], sm[:, 83:84]
            m1, m2, dlt, e2, den, p1, p2 = (sm[:, 84 + i:85 + i] for i in range(7))
            goh = sm[:, 96:104]
            ge = sm[:, 104:112]
            esel = sm[:, 112:120]
            oh1 = sm[:, 120:128]
            oh2 = sm[:, 128:136]
            msk = sm[:, 136:144]
            tmp3 = sm[:, 144:208].rearrange("p (g j) -> p g j", g=8)
            E1 = sm[:, 208:272]
            E2 = sm[:, 272:336]
            Es = sm[:, 336:400]
            slotf = sm[:, 400:464]
            d12 = sm[:, 464:466]
            if part == '1dve':
                V(lambda e: e.reduce_max(out=gmax, in_=gl, axis=AX.X), [sm], [sm])
            if part == '1dve':
                V(lambda e: e.tensor_scalar(out=goh, in0=gl, scalar1=gmax, scalar2=None, op0=ALU.is_equal), [sm], [sm])
            if part == '1dve':
                V(lambda e: e.tensor_scalar(out=ngmax, in0=gmax, scalar1=-1.0, scalar2=None, op0=ALU.mult), [sm], [sm])
            if part == '1dve':
                P.op("act", lambda e: e.activation(out=ge, in_=gl, func=AF.Exp, bias=ngmax, scale=1.0), reads=[sm], writes=[sm])
            if part == '1dve':
                V(lambda e: e.reduce_sum(out=gsum, in_=ge, axis=AX.X), [sm], [sm])
            if part == '1dve':
                V(lambda e: e.reciprocal(out=gprob, in_=gsum), [sm], [sm])
            if part == '1dve':
                V(lambda e: e.tensor_tensor(out=tmp3, in0=el3, in1=goh.unsqueeze(2).to_broadcast([128, 8, 8]), op=ALU.mult), [sm], [sm])
            if part == '1dve':
                V(lambda e: e.reduce_sum(out=esel, in_=tmp3.rearrange("p g j -> p j g"), axis=AX.X), [sm], [sm])
            if part == '1dve':
                V(lambda e: e.reduce_max(out=m1, in_=esel, axis=AX.X), [sm], [sm])
            if part == '1dve':
                V(lambda e: e.tensor_scalar(out=oh1, in0=esel, scalar1=m1, scalar2=None, op0=ALU.is_equal), [sm], [sm])
            if part == '1dve':
                V(lambda e: e.scalar_tensor_tensor(out=msk, in0=oh1, scalar=-1e30, in1=esel, op0=ALU.mult, op1=ALU.add), [sm], [sm])
            if part == '1dve':
                V(lambda e: e.reduce_max(out=m2, in_=msk, axis=AX.X), [sm], [sm])
            if part == '1dve':
                V(lambda e: e.tensor_scalar(out=oh2, in0=msk, scalar1=m2, scalar2=None, op0=ALU.is_equal), [sm], [sm])
            if part == '1dve':
                V(lambda e: e.tensor_tensor(out=dlt, in0=m2, in1=m1, op=ALU.subtract), [sm], [sm])
            if part == '1dve':
                P.op("act", lambda e: e.activation(out=e2, in_=dlt, func=AF.Exp), reads=[sm], writes=[sm])
            if part == '1dve':
                V(lambda e: e.tensor_scalar(out=den, in0=e2, scalar1=1.0, scalar2=None, op0=ALU.add), [sm], [sm])
            if part == '1dve':
                V(lambda e: e.reciprocal(out=p1, in_=den), [sm], [sm])
            if part == '1dve':
                V(lambda e: e.tensor_tensor(out=p2, in0=e2, in1=p1, op=ALU.mult), [sm], [sm])
            if part == '1dve':
                V(lambda e: e.tensor_tensor(out=gts[:, 2 * tt:2 * tt + 1], in0=gprob, in1=p1, op=ALU.mult), [sm], [gts])
            if part == '1dve':
                V(lambda e: e.tensor_tensor(out=gts[:, 2 * tt + 1:2 * tt + 2], in0=gprob, in1=p2, op=ALU.mult), [sm], [gts])
            gb3 = goh.unsqueeze(2).to_broadcast([128, 8, 8])
            if part == '1dve':
                V(lambda e: e.tensor_tensor(out=E1.rearrange("p (g j) -> p g j", g=8), in0=gb3,
                                            in1=oh1.unsqueeze(1).to_broadcast([128, 8, 8]), op=ALU.mult), [sm], [sm])
            if part == '1dve':
                V(lambda e: e.tensor_tensor(out=E2.rearrange("p (g j) -> p g j", g=8), in0=gb3,
                                            in1=oh2.unsqueeze(1).to_broadcast([128, 8, 8]), op=ALU.mult), [sm], [sm])
            if part == '1dve':
                V(lambda e: e.tensor_tensor(out=Es, in0=E1, in1=E2, op=ALU.add), [sm], [sm])
            if part == '1dve':
                V(lambda e: e.tensor_copy(out=eb[:], in_=Es), [sm], [eb])
            if part == '2pe':
                P.op("pe", lambda e: e.matmul(rkp[:, 0:64], lhsT=us[:], rhs=eb[:], start=True, stop=True), reads=[us, eb], writes=[rkp])
                P.op("pe", lambda e: e.matmul(rkp[:, 64:128], lhsT=ones_b[:], rhs=eb[:], start=True, stop=True), reads=[ones_b, eb], writes=[rkp])
            if part == '2dve':
                V(lambda e: e.tensor_tensor(out=slotf, in0=rkp[:, 0:64], in1=cntv[:], op=ALU.add), [rkp, cntv], [sm])
                V(lambda e: e.tensor_tensor(out=cntv[:], in0=rkp[:, 64:128], in1=cntv[:], op=ALU.add), [rkp, cntv], [cntv])
                V(lambda e: e.tensor_scalar(out=slotf, in0=slotf, scalar1=float(CAP - 1), scalar2=None, op0=ALU.min), [sm], [sm])
                V(lambda e: e.tensor_tensor(out=slotf, in0=slotf, in1=rr[:, 72:136], op=ALU.add), [sm, rr], [sm])
                V(lambda e: e.tensor_tensor(out=E1, in0=E1, in1=slotf, op=ALU.mult), [sm], [sm])
                V(lambda e: e.tensor_tensor(out=E2, in0=E2, in1=slotf, op=ALU.mult), [sm], [sm])
                V(lambda e: e.reduce_sum(out=d12[:, 0:1], in_=E1, axis=AX.X), [sm], [sm])
                V(lambda e: e.reduce_sum(out=d12[:, 1:2], in_=E2, axis=AX.X), [sm], [sm])
                V(lambda e: e.tensor_copy(out=dsti[:, 2 * tt:2 * tt + 2], in_=d12), [sm], [dsti])
                if "RT" in cfg.debug:
                    V(lambda e: e.tensor_copy(out=sm[:, 480:482], in_=d12), [sm], [sm])
                    V(lambda e: e.tensor_copy(out=sm[:, 482:484], in_=gts[:, 2 * tt:2 * tt + 2]), [sm, gts], [sm])
                    P.dma("sp", out=a["RT"][tsl, 0:4], in_=sm[:, 480:484], sb=sm, reads=[sm], writes=[a["RTb"]])
                for kk in range(2):
                    P.dma("pool", out=a["Xs"][:, :], in_=HB[:, :], sb=HB, reads=[HB, dsti, zt], writes=[a["Xsb"]],
                          indirect=dict(out_offset=bass.IndirectOffsetOnAxis(ap=dsti[:, 2 * tt + kk:2 * tt + kk + 1], axis=0),
                                        in_offset=None))


        load(0)
        if ntile > 1:
            load(1)
        stageA(0, 'pe')
        for i in range(ntile + 2):
            if i < ntile:
                stageA(i, 'res')
                a["cvt"]()
            if 1 <= i <= ntile:
                stageB(i - 1, '1pe')
            if 2 <= i:
                stageB(i - 2, '2pe')
            if i + 1 < ntile:
                stageA(i + 1, 'pe')
            if i + 2 < ntile:
                load(i + 2)
            if i < ntile:
                stageA(i, 'ln1')
                a["cvt"]()
            if 1 <= i <= ntile:
                stageB(i - 1, '1dve')
            if i < ntile:
                stageA(i, 'ln2')
            if 2 <= i:
                stageB(i - 2, '2dve')


def phase4(nc, P, cfg, sb, ps, a):
    ident_b = a["ident_b"]
    with ExitStack() as st:
        Wg = [sb("Wg%d" % i, [128, NCH, DE], BF16, st) for i in range(2)]
        Wu = [sb("Wu%d" % i, [128, NCH, DE], BF16, st) for i in range(2)]
        Wd = [sb("Wd%d" % i, [128, 4, D], BF16, st) for i in range(2)]
        Xe = [sb("Xe%d" % i, [128, D], BF16, st) for i in range(2)]
        XT = [sb("XT%d" % i, [128, NCH, 128], BF16, st) for i in range(2)]
        sg = sb("sg", [128, DE], F32, st)
        hid = sb("hid", [128, DE], BF16, st)
        hT = sb("hidT", [128, 4, 128], BF16, st)
        Yt = [sb("Yt%d" % i, [128, D], F32, st) for i in range(2)]
        tpx = [ps("tpx%d" % i, [128, 8, 128], BF16, st) for i in range(2)]
        gp = ps("gp", [128, DE], F32, st)
        up = ps("up", [128, DE], F32, st)
        tph = ps("tph", [128, 4, 128], BF16, st)
        dp = [ps("dp%d" % i, [128, 512], F32, st) for i in range(2)]
        kd = 0
        ncv = a["ncvt"]
        seq = []
        for k_ in range(NEXP // 2):
            seq += [k_, NEXP // 2 + k_]

        def load_w(pos):
            e_ = seq[pos]
            i = pos % 2
            if e_ < ncv:
                P.dma("sp", out=Wg[i][:], in_=a["WgB"][e_].rearrange("(p c) n -> p c n", c=NCH), sb=Wg[i],
                      reads=[a["WgBb"]], writes=[Wg[i]])
                P.dma("sp", out=Wu[i][:], in_=a["WuB"][e_].rearrange("(p c) n -> p c n", c=NCH), sb=Wu[i],
                      reads=[a["WuBb"]], writes=[Wu[i]])
                P.dma("sp", out=Wd[i][:], in_=a["WdB"][e_].rearrange("(c p) n -> p c n", p=128), sb=Wd[i],
                      reads=[a["WdBb"]], writes=[Wd[i]])
            else:
                P.dma("pool", out=Wg[i][:], in_=a["w_gate"][e_].rearrange("(p c) n -> p c n", c=NCH), sb=Wg[i], writes=[Wg[i]])
                P.dma("pool", out=Wu[i][:], in_=a["w_up"][e_].rearrange("(p c) n -> p c n", c=NCH), sb=Wu[i], writes=[Wu[i]])
                P.dma("pool", out=Wd[i][:], in_=a["w_down"][e_].rearrange("(c p) n -> p c n", p=128), sb=Wd[i], writes=[Wd[i]])

        load_w(0)
        for pos, e_ in enumerate(seq):
            i = pos % 2
            if pos + 1 < len(seq):
                load_w(pos + 1)
            X, XTt, Y = Xe[i], XT[i], Yt[i]
            P.dma("sp", out=X[:], in_=a["Xs"][e_ * CAP:(e_ + 1) * CAP, :], sb=X, reads=[a["Xsb"]], writes=[X])
            for half in range(2):
                TP = tpx[half]
                for j in range(8):
                    c = half * 8 + j
                    P.op("pe", lambda e: e.transpose(out=TP[:, j, :], in_=X[:, c * 128:(c + 1) * 128], identity=ident_b[:]),
                         reads=[X, ident_b], writes=[TP])
                if half == 0:
                    P.op("act", lambda e: e.copy(out=XTt[:, 0:8, :], in_=TP[:]), reads=[TP], writes=[XTt])
                else:
                    P.op("dve", lambda e: e.tensor_copy(out=XTt[:, 8:16, :], in_=TP[:]), reads=[TP], writes=[XTt])
            for c in range(NCH):
                P.op("pe", lambda e: e.matmul(gp[:], lhsT=XTt[:, c, :], rhs=Wg[i][:, c, :], start=(c == 0), stop=(c == NCH - 1)),
                     reads=[XTt, Wg[i]], writes=[gp])
            for c in range(NCH):
                P.op("pe", lambda e: e.matmul(up[:], lhsT=XTt[:, c, :], rhs=Wu[i][:, c, :], start=(c == 0), stop=(c == NCH - 1)),
                     reads=[XTt, Wu[i]], writes=[up])
            P.op("act", lambda e: e.activation(out=sg[:], in_=gp[:], func=AF.Silu), reads=[gp], writes=[sg])
            P.op("dve", lambda e: e.tensor_tensor(out=hid[:], in0=up[:], in1=sg[:], op=ALU.mult), reads=[up, sg], writes=[hid])
            for c in range(4):
                P.op("pe", lambda e: e.transpose(out=tph[:, c, :], in_=hid[:, c * 128:(c + 1) * 128], identity=ident_b[:]),
                     reads=[hid, ident_b], writes=[tph])
            P.op("act", lambda e: e.copy(out=hT[:], in_=tph[:]), reads=[tph], writes=[hT])
            for nb in range(4):
                DP = dp[kd % 2]
                kd += 1
                for c in range(4):
                    P.op("pe", lambda e: e.matmul(DP[:], lhsT=hT[:, c, :], rhs=Wd[i][:, c, nb * 512:(nb + 1) * 512],
                                                  start=(c == 0), stop=(c == 3)), reads=[hT, Wd[i]], writes=[DP])
                if nb % 2 == 0:
                    P.op("dve", lambda e: e.tensor_copy(out=Y[:, nb * 512:(nb + 1) * 512], in_=DP[:]), reads=[DP], writes=[Y])
                else:
                    P.op("act", lambda e: e.copy(out=Y[:, nb * 512:(nb + 1) * 512], in_=DP[:]), reads=[DP], writes=[Y])
            P.dma("sp", out=a["Ys"][e_ * CAP:(e_ + 1) * CAP, :], in_=Y[:], sb=Y, reads=[Y], writes=[a["Ysb"]])


def phase5(nc, P, cfg, sb, ps, a):
    dsti, gts = a["dsti"], a["gts"]
    with ExitStack() as st:
        G2v = sb("G2v", [128, D], F32, st)
        B2v = sb("B2v", [128, D], F32, st)
        rv = a["rowv"]
        P.dma("sp", out=G2v[:], in_=rv[:, 5 * D:6 * D].partition_broadcast(128), sb=G2v, writes=[G2v])
        P.dma("sp", out=B2v[:], in_=rv[:, 6 * D:7 * D].partition_broadcast(128), sb=B2v, writes=[B2v])
        Y1 = [sb("Y1_%d" % i, [128, D], F32, st) for i in range(3)]
        Y2 = [sb("Y2_%d" % i, [128, D], F32, st) for i in range(3)]
        Hh = [sb("Hh%d" % i, [128, D], F32, st) for i in range(3)]
        Oo = [sb("Oo%d" % i, [128, D], F32, st) for i in range(2)]
        stt = [sb("stt5_%d" % i, [128, 4, 6], F32, st) for i in range(2)]
        mvv = [sb("mv5_%d" % i, [128, 4], F32, st) for i in range(2)]
        epsc = sb("epsc5", [128, 1], F32, st)
        P.op("dve", lambda e: e.memset(epsc[:], LN_EPS), writes=[epsc])
        ntile = cfg.ntile3 if hasattr(cfg, "ntile3") else NTILE

        def load(tt):
            A, B, H = Y1[tt % 3], Y2[tt % 3], Hh[tt % 3]
            tsl = slice(tt * 128, (tt + 1) * 128)
            for kk, Yk in ((0, A), (1, B)):
                P.dma("pool", out=Yk[:, :], in_=a["Ys"][:, :], sb=Yk, reads=[a["Ysb"], dsti], writes=[Yk],
                      indirect=dict(out_offset=None,
                                    in_offset=bass.IndirectOffsetOnAxis(ap=dsti[:, 2 * tt + kk:2 * tt + kk + 1], axis=0)))
            P.dma("sp", out=H[:], in_=a["H1"][tsl, :], sb=H, reads=[a["H1b"]], writes=[H])

        def c1(tt):
            A, B, H = Y1[tt % 3], Y2[tt % 3], Hh[tt % 3]
            ST, mv = stt[tt % 2], mvv[tt % 2]
            P.op("act", lambda e: e.activation(out=A[:], in_=A[:], func=AF.Copy, scale=gts[:, 2 * tt:2 * tt + 1]),
                 reads=[A, gts], writes=[A])
            P.op("dve", lambda e: e.scalar_tensor_tensor(out=B[:], in0=B[:], scalar=gts[:, 2 * tt + 1:2 * tt + 2], in1=A[:],
                                                         op0=ALU.mult, op1=ALU.add), reads=[B, A, gts], writes=[B])
            P.op("dve", lambda e: e.scalar_tensor_tensor(out=H[:], in0=H[:], scalar=ALPHA, in1=B[:], op0=ALU.mult,
                                                         op1=ALU.add), reads=[H, B], writes=[H])
            for c in range(4):
                P.op("dve", lambda e, c=c: e.bn_stats(out=ST[:, c, :], in_=H[:, c * 512:(c + 1) * 512]), reads=[H], writes=[ST])
            P.op("dve", lambda e: e.bn_aggr(out=mv[:, 0:2], in_=ST[:]), reads=[ST], writes=[mv])
            P.op("act", lambda e: e.activation(out=mv[:, 3:4], in_=mv[:, 1:2], func=AF.Sqrt, bias=epsc[:, 0:1], scale=1.0),
                 reads=[mv, epsc], writes=[mv])
            P.op("dve", lambda e: e.reciprocal(out=mv[:, 1:2], in_=mv[:, 3:4]), reads=[mv], writes=[mv])
            P.op("dve", lambda e: e.scalar_tensor_tensor(out=mv[:, 2:3], in0=mv[:, 0:1], scalar=-1.0, in1=mv[:, 1:2],
                                                         op0=ALU.mult, op1=ALU.mult), reads=[mv], writes=[mv])

        def c2(tt):
            H, O, mv = Hh[tt % 3], Oo[tt % 2], mvv[tt % 2]
            tsl = slice(tt * 128, (tt + 1) * 128)
            P.op("act", lambda e: e.activation(out=O[:], in_=H[:], func=AF.Identity, scale=mv[:, 1:2], bias=mv[:, 2:3]),
                 reads=[H, mv], writes=[O])
            P.op("pool", lambda e: e.tensor_tensor(out=O[:], in0=O[:], in1=G2v[:], op=ALU.mult), reads=[O, G2v], writes=[O])
            P.op("dve", lambda e: e.tensor_tensor(out=O[:], in0=O[:], in1=B2v[:], op=ALU.add), reads=[O, B2v], writes=[O])
            P.dma("sp", out=a["out"][tsl, :], in_=O[:], sb=O, reads=[O], writes=[a["outb"]])

        load(0)
        if ntile > 1:
            load(1)
        c1(0)
        for tt in range(ntile):
            if tt + 2 < ntile:
                load(tt + 2)
            if tt + 1 < ntile:
                c1(tt + 1)
            c2(tt)


def KB_COL(g, t, r):
    off = (0, 2, 10)[g]
    return off + t * DIL[g] + r


def alibi_slopes(n):
    return np.array([2.0 ** (-8.0 * (i + 1) / n) for i in range(n)], dtype=np.float32)


def host_tables(rpb, q):
    rpb = np.asarray(rpb, np.float32)
    kc = np.arange(64)[:, None]
    qc = np.arange(64)[None, :]
    qcs = np.clip(qc - 8, 0, 48)
    colvalid = (kc >= qcs) & (kc < qcs + 16)
    dc = np.clip(kc - qc + 15, 0, 30)
    tab = np.full((NTAB, NA_H, 128, 128), NEG, np.float32)
    for rp in (0, 1, 7, 14, 15):
        cls = na_class(rp)
        base, offs = NA_CLASSES[cls]
        for i, o in enumerate(offs):
            for aa in range(2):
                for bb in range(2):
                    r = q * 32 + 2 * rp + bb
                    R = q * 32 + 2 * (rp + o) + aa
                    st = min(max(r - 4, 0), 120)
                    if not (0 <= R <= 127 and st <= R <= st + 7):
                        continue
                    blk = np.where(colvalid[None], rpb[:, R - r + 7][:, dc], NEG)
                    tab[base + i, :, aa * 64:(aa + 1) * 64, bb * 64:(bb + 1) * 64] = blk
    sl = alibi_slopes(12)
    aidx = np.arange(128)[:, None]
    iidx = np.arange(256)[None, :]
    rel = aidx + 64 - iidx
    ab = np.empty((12, 128, 256), np.float32)
    for hd in range(12):
        d = DIL[hd // 4]
        ab[hd] = np.where(np.abs(rel) <= 64, -sl[hd] * (np.abs(rel) * d).astype(np.float32), NEG)
    kb = np.zeros((128, 96), np.float32)
    apos = np.arange(128)
    for g, d in enumerate(DIL):
        Lq = TOK // d
        for t in range(2):
            for r in range(d):
                m = 0 if t == 0 else Lq // 128
                tok = (m * 128 - 64 + apos) * d + r + q * TOK
                kb[:, KB_COL(g, t, r)] = np.where((tok >= 0) & (tok < 8192), 0.0, NEG)
    return tab.reshape(NTAB * NA_H, 128, 128), ab, kb


def f(a):
    return np.asarray(a, np.float32)


def host_inputs(inp, core):
    b, q = divmod(core, 4)
    x = np.asarray(inp["x"], np.float32)
    xhal = np.zeros((TOK + 2 * HALO, D), np.float32)
    lo = q * TOK - HALO
    hi = q * TOK + TOK + HALO
    a, z = max(lo, 0), min(hi, 8192)
    xhal[a - lo:z - lo] = x[b, a:z]
    cvec = np.zeros((128, 160), np.float32)
    cvec[:, 0:16] = np.asarray(inp["ln0_g"], np.float32).reshape(16, 128).T
    cvec[:, 16:32] = np.asarray(inp["ln0_b"], np.float32).reshape(16, 128).T
    b_in = np.asarray(inp["b_in"], np.float32)[0]
    cvec[:, 32:124] = b_in.reshape(92, 128).T
    bv = np.concatenate([b_in[2048:3072], b_in[6144:7680]])[None, :]
    tabs = host_tables(np.asarray(inp["rpb"])[0], q)
    return {
        "xh": xhal,
        "w_in": np.asarray(inp["w_in"], np.float32)[0],
        "cvec": cvec,
        "bv": np.ascontiguousarray(bv),
        "identf": np.eye(128, dtype=np.float32),
        "natab": tabs[0], "abt": tabs[1], "kbias": tabs[2],
        "wpa": f(inp["w_proj_a"])[0], "wpb": f(inp["w_proj_b"])[0], "wo": f(inp["w_o"])[0],
        "rowv": np.concatenate([f(inp["ln0_g"]), f(inp["ln0_b"]), f(inp["b_o"])[0], f(inp["ln1_g"])[0],
                                f(inp["ln1_b"])[0], f(inp["ln2_g"])[0], f(inp["ln2_b"])[0]])[None, :],
        "wr": np.concatenate([f(inp["w_router_group"])[0], f(inp["w_router_expert"])[0]], axis=1),
        "rowr": np.concatenate([f(inp["b_router_group"])[0], f(inp["b_router_expert"])[0],
                                (np.arange(64) * CAP).astype(np.float32)])[None, :],
        "ustr": np.triu(np.ones((128, 128), np.float32), 1),
        "w_gate": f(inp["w_gate"])[0], "w_up": f(inp["w_up"])[0], "w_down": f(inp["w_down"])[0],
    }


def kernel(**inp):
    cfg = Cfg()
    nc = build(cfg)
    in_maps = [host_inputs(inp, c) for c in range(NCORES)]
    res = run_bass_kernel_spmd(nc, in_maps, core_ids=list(range(NCORES)))
    out = np.zeros((2, 8192, D), np.float32)
    for c in range(NCORES):
        b, q = divmod(c, 4)
        out[b, q * TOK:(q + 1) * TOK] = res.results[c]["out"]
    return out
```

```python
from contextlib import ExitStack

import numpy as np

import concourse.bass as bass
import concourse.mybir as mybir
from concourse.bass_utils import run_bass_kernel_spmd

F32 = mybir.dt.float32
BF16 = mybir.dt.bfloat16
I32 = mybir.dt.int32
AF = mybir.ActivationFunctionType
ALU = mybir.AluOpType
AX = mybir.AxisListType

NCORES = 8
D = 2048
NCH = D // 128
TOK = 2048
HALO = 1024
NTILE = TOK // 128
D_IN = 11776
NA_H = 8
DIL = (1, 4, 16)
NEG = -30000.0
LN_EPS = 1e-5
ALPHA = 2.0 ** 0.25
SCALE = 128.0 ** -0.5
NEXP = 64
DE = 512
CAP = 128


class Buf:
    __slots__ = ("name", "w", "r", "sem", "dma_total", "multi", "wset")

    def __init__(self, name, multi=False):
        self.name = name
        self.w = None
        self.r = {}
        self.sem = None
        self.dma_total = 0
        self.multi = multi
        self.wset = {}


class T:
    def __init__(self, h, name):
        self.h = h
        self.buf = Buf(name)

    def __getitem__(self, k):
        return self.h[k]


class Prog:
    def __init__(self, nc):
        self.nc = nc
        self.e = {"pe": nc.tensor, "act": nc.scalar, "dve": nc.vector, "pool": nc.gpsimd, "sp": nc.sync}
        self.sem = {k: nc.alloc_semaphore("sem_" + k) for k in ("pe", "act", "dve", "pool")}
        self.cnt = {k: 0 for k in self.sem}
        self.waited = {k: {} for k in self.e}
        self.dbufs = []
        self.nsem = 0

    def _wait(self, eng, t):
        if t is None:
            return
        if t[0] == "c":
            key, val, sem = t[1], t[2], self.sem[t[1]]
            if eng == "pe" and key == "pe":
                return
        else:
            b = t[1]
            key, val, sem = ("d", id(b)), b.dma_total, b.sem
        if self.waited[eng].get(key, 0) >= val:
            return
        self.e[eng].wait_ge(sem, val)
        self.waited[eng][key] = val

    def _deps(self, eng, reads, writes, nw=False):
        for b in reads:
            if b.multi:
                for t in b.wset.values():
                    self._wait(eng, t)
            else:
                self._wait(eng, b.w)
        for b in writes:
            if not b.multi:
                if not (nw and b.w is not None and b.w[0] == "c" and b.w[1] == eng):
                    self._wait(eng, b.w)
            for t in b.r.values():
                self._wait(eng, t)

    def _mark(self, t, rkey, reads, writes):
        for b in reads:
            b.r[rkey] = t
        for b in writes:
            if b.multi:
                b.wset[rkey] = t
            else:
                b.w = t
            b.r = {}

    def op(self, eng, fn, reads=(), writes=(), nw=False):
        reads = [x.buf if isinstance(x, T) else x for x in reads]
        writes = [x.buf if isinstance(x, T) else x for x in writes]
        self._deps(eng, reads, writes, nw)
        ins = fn(self.e[eng])
        self.cnt[eng] += 1
        ins.then_inc(self.sem[eng], 1)
        self._mark(("c", eng, self.cnt[eng]), eng, reads, writes)

    def dma(self, q, out, in_, sb, reads=(), writes=(), indirect=None, **kw):
        reads = [x.buf if isinstance(x, T) else x for x in reads]
        writes = [x.buf if isinstance(x, T) else x for x in writes]
        sb = sb.buf if isinstance(sb, T) else sb
        self._deps(q, reads, writes)
        if sb.sem is None:
            sb.sem = self.nc.alloc_semaphore("dsem_%d" % self.nsem)
            self.nsem += 1
            self.dbufs.append(sb)
        if indirect is None:
            ins = self.e[q].dma_start(out=out, in_=in_, **kw)
        else:
            ins = self.e[q].indirect_dma_start(out=out, in_=in_, **indirect)
        ins.then_inc(sb.sem, 16)
        sb.dma_total += 16
        self._mark(("d", sb), ("d", id(sb)), reads, writes)

    def barrier(self):
        for eng in self.e:
            for k in self.sem:
                if self.cnt[k]:
                    self._wait(eng, ("c", k, self.cnt[k]))
            for b in self.dbufs:
                self._wait(eng, ("d", b))


class Cfg:
    def __init__(self, **kw):
        self.stage = 99
        self.debug = ()
        self.gather_w = False
        self.do_na = True
        self.do_dil = True
        self.nexp = NEXP
        self.__dict__.update(kw)


def build(cfg):
    nc = bass.Bass("TRN2", target_bir_lowering=False)
    P = Prog(nc)
    dbg = set(cfg.debug)

    def dram_in(name, shape, dt=F32):
        return nc.dram_tensor(name, list(shape), dt, kind="ExternalInput").ap()

    def scratch(name, shape, dt=BF16):
        kind = "ExternalOutput" if name in dbg else "Internal"
        h = nc.dram_tensor(name, list(shape), dt, kind=kind).ap()
        return h, Buf(name, multi=True)

    xh = dram_in("xh", [TOK + 2 * HALO, D])
    w_in = dram_in("w_in", [D, D_IN])
    cvec = dram_in("cvec", [128, 160])
    bv = dram_in("bv", [1, 2560])
    identf = dram_in("identf", [128, 128])
    natab = dram_in("natab", [NTAB * NA_H, 128, 128])
    abt = dram_in("abt", [12, 128, 256])
    kbias = dram_in("kbias", [128, 96])
    OTd, OTdb = scratch("OTd", [12, 128, TOK])
    wpa = dram_in("wpa", [1024, D])
    wpb = dram_in("wpb", [512, D])
    wo = dram_in("wo", [D, D])
    rowv = dram_in("rowv", [1, 7 * D])
    wr = dram_in("wr", [D, 72])
    rowr = dram_in("rowr", [1, 72 + 64])
    ustr = dram_in("ustr", [128, 128])
    ne = cfg.nexp
    w_gate = dram_in("w_gate", [ne, D, DE])
    w_up = dram_in("w_up", [ne, D, DE])
    w_down = dram_in("w_down", [ne, DE, D])
    MT, MTb = scratch("MT", [NCH, 128, TOK])
    H1, H1b_ = scratch("H1", [TOK, D], F32)
    Xs, Xsb = scratch("Xs", [NEXP * CAP, D])
    Ys, Ysb = scratch("Ys", [NEXP * CAP, D], F32)
    RT, RTb = scratch("RT", [TOK, 8], F32)
    out_d = nc.dram_tensor("out", [TOK, D], F32, kind="ExternalOutput").ap()
    out_b = Buf("out", multi=True)

    QA, QAb = scratch("QA", [NA_H, 128, TOK])
    KA, KAb = scratch("KA", [NA_H, 128, TOK + 512])
    VA, VAb = scratch("VA", [TOK + 512, 1024])
    QB, QBb, KB, KBb, VB, VBb = [], [], [], [], [], []
    for g, d in enumerate(DIL):
        a, b = scratch("QB%d" % g, [4, 128, TOK]); QB.append(a); QBb.append(b)
        a, b = scratch("KB%d" % g, [4, 128, d, TOK // d + 128]); KB.append(a); KBb.append(b)
        hal = 256 if g < 2 else 1024
        a, b = scratch("VB%d" % g, [TOK + 2 * hal, 512]); VB.append(a); VBb.append(b)
    GA, GAb = scratch("GA", [NCH, 128, TOK])
    GB, GBb = scratch("GB", [NCH, 128, TOK])

    outs = {}
    es = ExitStack()

    def sb(name, shape, dt, stack=es):
        return T(stack.enter_context(nc.sbuf_tensor(name, list(shape), dt)), name)

    def ps(name, shape, dt, stack=es):
        return T(stack.enter_context(nc.psum_tensor(name, list(shape), dt)), name)

    with es:
        cv = sb("cv", [128, 160], F32)
        P.dma("sp", out=cv[:], in_=cvec, sb=cv, writes=[cv])
        ident_b = sb("ident_b", [128, 128], BF16)
        P.dma("pool", out=ident_b[:], in_=identf, sb=ident_b, writes=[ident_b])
        g0T = cv[:, 0:16]
        b0T = cv[:, 16:32]
        binT = cv[:, 32:124]
        epsc = sb("epsc", [128, 1], F32)
        P.op("dve", lambda e: e.memset(epsc[:], LN_EPS), writes=[epsc])
        bq = sb("bq", [128, 92], F32)
        P.op("dve", lambda e: e.tensor_scalar(out=bq[:], in0=binT, scalar1=SCALE, scalar2=None, op0=ALU.mult),
             reads=[cv], writes=[bq])

        with ExitStack() as s1:
            hT = sb("hT", [128, NCH, TOK], BF16, s1)
            xt = [sb("xt%d" % i, [128, D], F32, s1) for i in range(2)]
            xn = [sb("xn%d" % i, [128, D], BF16, s1) for i in range(2)]
            stt = [sb("stt%d" % i, [128, 4, 6], F32, s1) for i in range(2)]
            mv = [sb("mv%d" % i, [128, 4], F32, s1) for i in range(2)]
            wb = [sb("wb%d" % i, [128, NCH, 512], BF16, s1) for i in range(2)]
            stg = [sb("stg%d" % i, [128, TOK], BF16, s1) for i in range(2)]
            vst = [sb("vst%d" % i, [128, 512], BF16, s1) for i in range(3)]
            bvb = sb("bvb", [128, 2560], F32, s1)
            P.dma("sp", out=bvb[:], in_=bv.partition_broadcast(128), sb=bvb, writes=[bvb])
            tp = [ps("tp%d" % i, [128, 8, 128], BF16, s1) for i in range(2)]
            acc = [ps("acc%d" % i, [128, 512], F32, s1) for i in range(4)]
            cnt = {"x": 0, "acc": 0, "stg": 0, "vst": 0, "wb": 0}

            def ln0_tiles(rows, slots):
                for r0, s0 in zip(rows, slots):
                    i = cnt["x"] % 2
                    cnt["x"] += 1
                    X, XN, ST, MV = xt[i], xn[i], stt[i], mv[i]
                    P.dma("sp", out=X[:], in_=xh[r0:r0 + 128, :], sb=X, writes=[X])
                    for c in range(4):
                        P.op("dve", lambda e, c=c: e.bn_stats(out=ST[:, c, :], in_=X[:, c * 512:(c + 1) * 512]),
                             reads=[X], writes=[ST])
                    P.op("dve", lambda e: e.bn_aggr(out=MV[:, 0:2], in_=ST[:]), reads=[ST], writes=[MV])
                    P.op("act", lambda e: e.activation(out=MV[:, 3:4], in_=MV[:, 1:2], func=AF.Sqrt, bias=epsc[:, 0:1],
                                                        scale=1.0), reads=[MV, epsc], writes=[MV])
                    P.op("dve", lambda e: e.reciprocal(out=MV[:, 2:3], in_=MV[:, 3:4]), reads=[MV], writes=[MV])
                    P.op("dve", lambda e: e.tensor_scalar(out=XN[:], in0=X[:], scalar1=MV[:, 0:1],
                                                           scalar2=MV[:, 2:3], op0=ALU.subtract, op1=ALU.mult),
                         reads=[X, MV], writes=[XN])
                    for half in range(2):
                        TP = tp[half]
                        for j in range(8):
                            c = half * 8 + j
                            P.op("pe", lambda e, c=c, j=j: e.transpose(out=TP[:, j, :], in_=XN[:, c * 128:(c + 1) * 128],
                                                                        identity=ident_b[:]),
                                 reads=[XN, ident_b], writes=[TP])
                        for j in range(8):
                            c = half * 8 + j
                            P.op("act", lambda e, c=c, j=j: e.activation(out=hT[:, c, s0:s0 + 128], in_=TP[:, j, :],
                                                                          func=AF.Identity, scale=g0T[:, c:c + 1],
                                                                          bias=b0T[:, c:c + 1]),
                                 reads=[TP, cv], writes=[hT], nw=True)

            def load_w(cb):
                W = wb[cnt["wb"] % 2]
                cnt["wb"] += 1
                src = w_in[:, cb * 512:(cb + 1) * 512].rearrange("(c p) n -> p c n", p=128)
                P.dma("pool", out=W[:], in_=src, sb=W, writes=[W])
                return W

            def fm_block(W, hh, groups, evac):
                for (s0, n) in groups:
                    A = acc[cnt["acc"] % 4]
                    cnt["acc"] += 1
                    for c in range(NCH):
                        P.op("pe", lambda e, c=c: e.matmul(A[:, 0:n], lhsT=W[:, c, hh * 128:(hh + 1) * 128],
                                                            rhs=hT[:, c, s0:s0 + n], start=(c == 0), stop=(c == NCH - 1)),
                             reads=[W, hT], writes=[A])
                    evac(A, s0, n)

            def tm_block(W, tiles, evac):
                for s0 in tiles:
                    A = acc[cnt["acc"] % 4]
                    cnt["acc"] += 1
                    for c in range(NCH):
                        P.op("pe", lambda e, c=c: e.matmul(A[:], lhsT=hT[:, c, s0:s0 + 128], rhs=W[:, c, :],
                                                            start=(c == 0), stop=(c == NCH - 1)),
                             reads=[W, hT], writes=[A])
                    evac(A, s0)

            def next_stg():
                S = stg[cnt["stg"] % 2]
                cnt["stg"] += 1
                return S

            def v_evac(dst, dstb, boff, rowmap):
                def f(A, s0):
                    V = vst[cnt["vst"] % 3]
                    cnt["vst"] += 1
                    P.op("dve", lambda e: e.tensor_tensor(out=V[:], in0=A[:], in1=bvb[:, boff:boff + 512], op=ALU.add),
                         reads=[A, bvb], writes=[V])
                    r = rowmap(s0)
                    P.dma("sp", out=dst[r:r + 128, :] if dst.shape[1] == 512 else dst[r:r + 128, boff:boff + 512],
                          in_=V[:], sb=V, reads=[V], writes=[dstb])
                return f

            ln0_tiles([HALO + i * 128 for i in range(NTILE)], [i * 128 for i in range(NTILE)])
            own_groups = [(tb * 512, 512) for tb in range(4)]
            own_tiles = [i * 128 for i in range(NTILE)]

            def passA_block(cb):
                W = load_w(cb)
                if cb in (0, 1, 2, 3):
                    for hh in range(4):
                        h = (cb % 2) * 4 + hh
                        col = cb * 4 + hh
                        S = next_stg()
                        if cb < 2:
                            ev = lambda A, s0, n, S=S, col=col: P.op(
                                "act", lambda e: e.activation(out=S[:, s0:s0 + n], in_=A[:, 0:n], func=AF.Identity,
                                                              scale=SCALE, bias=bq[:, col:col + 1]),
                                reads=[A, bq], writes=[S])
                        else:
                            ev = lambda A, s0, n, S=S, col=col: P.op(
                                "act", lambda e: e.activation(out=S[:, s0:s0 + n], in_=A[:, 0:n], func=AF.Identity,
                                                              scale=1.0, bias=binT[:, col:col + 1]),
                                reads=[A, cv], writes=[S])
                        fm_block(W, hh, own_groups, ev)
                        if cb < 2:
                            P.dma("sp", out=QA[h], in_=S[:], sb=S, reads=[S], writes=[QAb])
                        else:
                            P.dma("sp", out=KA[h][:, 256:256 + TOK], in_=S[:], sb=S, reads=[S], writes=[KAb])
                elif cb in (4, 5):
                    boff = (cb - 4) * 512
                    tm_block(W, own_tiles, v_evac(VA, VAb, boff, lambda s0: s0 + 256))
                elif 6 <= cb <= 11:
                    isq = cb < 9
                    g = (cb - 6) % 3
                    d = DIL[g]
                    for hh in range(4):
                        col = cb * 4 + hh
                        S = next_stg()
                        Sv = S[:].rearrange("p (r l) -> p r l", r=d)

                        def ev(A, s0, n, Sv=Sv, S=S, col=col, d=d, isq=isq):
                            l0 = s0 // d
                            src = A[:, 0:n].rearrange("p (l r) -> p r l", r=d)
                            P.op("act", lambda e: e.activation(
                                out=Sv[:, :, l0:l0 + n // d], in_=src, func=AF.Identity,
                                scale=(SCALE if isq else 1.0),
                                bias=(bq[:, col:col + 1] if isq else binT[:, col:col + 1])),
                                reads=[A, bq, cv], writes=[S])
                        fm_block(W, hh, own_groups, ev)
                        if isq:
                            P.dma("sp", out=QB[g][hh], in_=S[:], sb=S, reads=[S], writes=[QBb[g]])
                        else:
                            P.dma("sp", out=KB[g][hh][:, :, 64:64 + TOK // d], in_=Sv, sb=S, reads=[S], writes=[KBb[g]])
                elif 12 <= cb <= 14:
                    g = cb - 12
                    hal = 256 if g < 2 else 1024
                    boff = 1024 + g * 512
                    tm_block(W, own_tiles, v_evac(VB[g], VBb[g], boff, lambda s0, hal=hal: s0 + hal))
                else:
                    isa = cb < 19
                    for hh in range(4):
                        col = cb * 4 + hh
                        ch = (cb - (15 if isa else 19)) * 4 + hh
                        S = next_stg()
                        ev = lambda A, s0, n, S=S, col=col: P.op(
                            "act", lambda e: e.activation(out=S[:, s0:s0 + n], in_=A[:, 0:n], func=AF.Sigmoid,
                                                          scale=1.0, bias=binT[:, col:col + 1]),
                            reads=[A, cv], writes=[S])
                        fm_block(W, hh, own_groups, ev)
                        P.dma("sp", out=(GA if isa else GB)[ch], in_=S[:], sb=S, reads=[S],
                              writes=[GAb if isa else GBb])

            blocksA = cfg.blocksA if hasattr(cfg, "blocksA") else list(range(23))
            for cb in blocksA:
                passA_block(cb)

            if cfg.stage >= 1:
                rows = ([HALO - 256, HALO - 128] + [HALO + TOK, HALO + TOK + 128]
                        + [i * 128 for i in range(6)] + [HALO + TOK + 256 + i * 128 for i in range(6)])
                ln0_tiles(rows, [i * 128 for i in range(16)])

                def slot_tok(s0):
                    if s0 < 256:
                        return s0 - 256
                    if s0 < 512:
                        return TOK + (s0 - 256)
                    if s0 < 1280:
                        return -1024 + (s0 - 512)
                    return TOK + 256 + (s0 - 1280)

                near = [(0, 512)]
                allg = [(0, 512), (512, 512), (1024, 512), (1536, 512)]
                for cb in (2, 3, 4, 5, 9, 10, 11, 12, 13, 14):
                    W = load_w(cb)
                    if cb in (2, 3):
                        for hh in range(4):
                            h = (cb % 2) * 4 + hh
                            col = cb * 4 + hh
                            S = next_stg()
                            ev = lambda A, s0, n, S=S, col=col: P.op(
                                "act", lambda e: e.activation(out=S[:, s0:s0 + n], in_=A[:, 0:n], func=AF.Identity,
                                                              scale=1.0, bias=binT[:, col:col + 1]),
                                reads=[A, cv], writes=[S])
                            fm_block(W, hh, near, ev)
                            P.dma("sp", out=KA[h][:, 0:256], in_=S[:, 0:256], sb=S, reads=[S], writes=[KAb])
                            P.dma("sp", out=KA[h][:, 256 + TOK:512 + TOK], in_=S[:, 256:512], sb=S, reads=[S], writes=[KAb])
                    elif cb in (4, 5):
                        boff = (cb - 4) * 512
                        tm_block(W, [0, 128, 256, 384], v_evac(VA, VAb, boff, lambda s0: slot_tok(s0) + 256))
                    elif 9 <= cb <= 11:
                        g = cb - 9
                        d = DIL[g]
                        for hh in range(4):
                            col = cb * 4 + hh
                            S = next_stg()
                            if g < 2:
                                nl = 256 // d
                                Sv = S[:, 0:512].rearrange("p (s r l) -> p s r l", s=2, r=d)

                                def ev(A, s0, n, Sv=Sv, S=S, col=col, d=d, nl=nl):
                                    src = A[:, 0:512].rearrange("p (s l r) -> p s r l", s=2, r=d)
                                    for s_ in range(2):
                                        P.op("act", lambda e, s_=s_: e.activation(
                                            out=Sv[:, s_, :, :], in_=src[:, s_, :, :], func=AF.Identity, scale=1.0,
                                            bias=binT[:, col:col + 1]), reads=[A, cv], writes=[S])
                                fm_block(W, hh, near, ev)
                                k = min(nl, 64)
                                P.dma("sp", out=KB[g][hh][:, :, 64 - k:64], in_=Sv[:, 0, :, nl - k:nl], sb=S,
                                      reads=[S], writes=[KBb[g]])
                                P.dma("sp", out=KB[g][hh][:, :, 64 + TOK // d:64 + TOK // d + k], in_=Sv[:, 1, :, 0:k],
                                      sb=S, reads=[S], writes=[KBb[g]])
                            else:
                                Sv = S[:].rearrange("p (s r l) -> p s r l", s=2, r=16)

                                def ev(A, s0, n, Sv=Sv, S=S, col=col):
                                    t0 = slot_tok(s0)
                                    side = 0 if t0 < 0 else 1
                                    off = (t0 + 1024) if side == 0 else (t0 - TOK)
                                    if s0 == 0:
                                        for s_, o in ((0, 768), (1, 0)):
                                            src = A[:, s_ * 256:(s_ + 1) * 256].rearrange("p (l r) -> p r l", r=16)
                                            P.op("act", lambda e, s_=s_, o=o, src=src: e.activation(
                                                out=Sv[:, s_, :, o // 16:o // 16 + 16], in_=src, func=AF.Identity,
                                                scale=1.0, bias=binT[:, col:col + 1]), reads=[A, cv], writes=[S])
                                        return
                                    pieces = []
                                    a = s0
                                    while a < s0 + n:
                                        bnd = 1280 if a < 1280 else 2048
                                        b = min(s0 + n, bnd)
                                        pieces.append((a, b))
                                        a = b
                                    for (a, b) in pieces:
                                        t0 = slot_tok(a)
                                        side = 0 if t0 < 0 else 1
                                        off = (t0 + 1024) if side == 0 else (t0 - TOK)
                                        src = A[:, a - s0:b - s0].rearrange("p (l r) -> p r l", r=16)
                                        P.op("act", lambda e, side=side, off=off, src=src, a=a, b=b: e.activation(
                                            out=Sv[:, side, :, off // 16:off // 16 + (b - a) // 16], in_=src,
                                            func=AF.Identity, scale=1.0, bias=binT[:, col:col + 1]),
                                            reads=[A, cv], writes=[S])
                                fm_block(W, hh, allg, ev)
                                P.dma("sp", out=KB[g][hh][:, :, 0:64], in_=Sv[:, 0], sb=S, reads=[S], writes=[KBb[g]])
                                P.dma("sp", out=KB[g][hh][:, :, 192:256], in_=Sv[:, 1], sb=S, reads=[S], writes=[KBb[g]])
                    else:
                        g = cb - 12
                        hal = 256 if g < 2 else 1024
                        boff = 1024 + g * 512
                        tiles = [0, 128, 256, 384] if g < 2 else [i * 128 for i in range(16)]
                        tm_block(W, tiles, v_evac(VB[g], VBb[g], boff, lambda s0, hal=hal: slot_tok(s0) + hal))
            P.barrier()

        zt = sb("zt", [128, D], BF16)
        P.op("pool", lambda e: e.memset(zt[:], 0.0), writes=[zt])
        for e_ in range(NEXP):
            P.dma("sp", out=Xs[e_ * CAP:(e_ + 1) * CAP, :], in_=zt[:], sb=zt, reads=[zt], writes=[Xsb])
        ident_f = sb("ident_f", [128, 128], F32)
        P.dma("sp", out=ident_f[:], in_=identf, sb=ident_f, writes=[ident_f])
        ones_b = sb("ones_b", [128, 128], BF16)
        P.op("dve", lambda e: e.memset(ones_b[:], 1.0), writes=[ones_b])
        dsti = sb("dsti", [128, 32], I32)
        gts = sb("gts", [128, 32], F32)
        if cfg.stage >= 2:
            with ExitStack() as sA:
                OT = sb("OT", [128, 12, TOK], BF16, sA)
                phase2(nc, P, cfg, sb, ps, dict(QA=QA, QAb=QAb, KA=KA, KAb=KAb, VA=VA, VAb=VAb, QB=QB, QBb=QBb, KB=KB,
                                                KBb=KBb, VB=VB, VBb=VBb, natab=natab, abt=abt, kbias=kbias,
                                                ident_b=ident_b, ones_b=ones_b, OT=OT))
                if "OTd" in dbg:
                    for h in range(12):
                        P.dma("sp", out=OTd[h], in_=OT[:, h, :], sb=OT, reads=[OT], writes=[OTdb])
                if cfg.stage >= 3:
                    phase3a(nc, P, cfg, sb, ps, dict(OT=OT, wpa=wpa, wpb=wpb, GA=GA, GAb=GAb, GB=GB, GBb=GBb, MT=MT, MTb=MTb))
                P.barrier()
        if cfg.stage >= 4:
            phase3b(nc, P, cfg, sb, ps, dict(MT=MT, MTb=MTb, wo=wo, rowv=rowv, wr=wr, rowr=rowr, ustr=ustr, xh=xh,
                                             ident_f=ident_f, ones_b=ones_b, H1=H1, H1b=H1b_, Xs=Xs, Xsb=Xsb, zt=zt,
                                             dsti=dsti, gts=gts, RT=RT, RTb=RTb))
            P.barrier()
        if cfg.stage >= 5:
            phase4(nc, P, cfg, sb, ps, dict(w_gate=w_gate, w_up=w_up, w_down=w_down, Xs=Xs, Xsb=Xsb, Ys=Ys, Ysb=Ysb,
                                            ident_b=ident_b))
            P.barrier()
        if cfg.stage >= 6:
            phase5(nc, P, cfg, sb, ps, dict(Ys=Ys, Ysb=Ysb, H1=H1, H1b=H1b_, rowv=rowv, dsti=dsti, gts=gts,
                                            out=out_d, outb=out_b))
            P.barrier()
        P.barrier()
    return nc


NA_CLASSES = {
    "gen": (0, (-2, -1, 0, 1, 2)),
    "p0": (5, (-2, -1, 0, 1, 2, 3)),
    "p1": (11, (-2, -1, 0, 1, 2)),
    "p14": (16, (-2, -1, 0, 1, 2)),
    "p15": (21, (-3, -2, -1, 0, 1, 2)),
}
NTAB = 27


def na_class(rp):
    return {0: "p0", 1: "p1", 14: "p14", 15: "p15"}.get(rp, "gen")


class Item:
    __slots__ = ("S", "E", "V", "N", "post")

    def __init__(self):
        self.post = None


def phase2(nc, P, cfg, sb, ps, a):
    OT = a["OT"]
    ident_b, ones_b = a["ident_b"], a["ones_b"]
    with ExitStack() as s2:
        nat = sb("nat", [128, NTAB * NA_H, 128], BF16, s2)
        half = NTAB * NA_H // 2
        for i in range(2):
            P.dma("pool", out=nat[:, i * half:(i + 1) * half, :],
                  in_=a["natab"][i * half:(i + 1) * half].rearrange("t p q -> p t q"), sb=nat, writes=[nat])
        abt = sb("abt_s", [128, 12, 256], BF16, s2)
        P.dma("pool", out=abt[:], in_=a["abt"].rearrange("h p q -> p h q"), sb=abt, writes=[abt])
        kbs = sb("kbs", [128, 96], F32, s2)
        P.dma("sp", out=kbs[:], in_=a["kbias"], sb=kbs, writes=[kbs])
        qT = [sb("qT%d" % i, [128, TOK], BF16, s2) for i in range(2)]
        kT = [sb("kT%d" % i, [128, 4096], BF16, s2) for i in range(2)]
        vt = [sb("vt%d" % i, [128, 32, 128], BF16, s2) for i in range(2)]
        PT = [sb("PT%d" % i, [128, 768], BF16, s2) for i in range(3)]
        rD = [sb("rD%d" % i, [128, 128], F32, s2) for i in range(2)]
        Uacc = sb("Uacc", [128, TOK], F32, s2)
        Dacc = sb("Dacc", [128, TOK], F32, s2)
        SA = [ps("SA%d" % i, [128, 1024], F32, s2) for i in range(2)]
        OB = [ps("OB%d" % i, [128, 512], F32, s2) for i in range(3)]
        groups = []
        kk = [0]

        def nextk():
            k = kk[0]
            kk[0] += 1
            return k

        for h in (range(NA_H) if cfg.do_na else ()):
            gi = len(groups)
            Q, Kt, V = qT[gi % 2], kT[gi % 2], vt[gi % 2]

            def load(h=h, Q=Q, Kt=Kt, V=V):
                P.dma("sp", out=Q[:], in_=a["QA"][h], sb=Q, reads=[a["QAb"]], writes=[Q])
                P.dma("sp", out=Kt[:, 0:TOK + 512], in_=a["KA"][h], sb=Kt, reads=[a["KAb"]], writes=[Kt])
                P.dma("sp", out=V[:, 0:20, :], in_=a["VA"][:, h * 128:(h + 1) * 128].rearrange("(t p) c -> p t c", p=128),
                      sb=V, reads=[a["VAb"]], writes=[V])
            items = []
            for rp in range(16):
                base, offs = NA_CLASSES[na_class(rp)]
                nt = len(offs)
                k = nextk()
                S, Pt, O, R = SA[k % 2], PT[k % 3], OB[k % 3], rD[k % 2]
                qs = slice(rp * 128, (rp + 1) * 128)
                it = Item()

                def fS(S=S, Q=Q, Kt=Kt, offs=offs, rp=rp, base=base, h=h, qs=qs):
                    for i, o in enumerate(offs):
                        kp = rp + o + 2
                        cs = slice(i * 128, (i + 1) * 128)
                        P.op("pe", lambda e: e.matmul(S[:, cs], lhsT=Kt[:, kp * 128:(kp + 1) * 128], rhs=Q[:, qs],
                                                      start=True, stop=False), reads=[Kt, Q], writes=[S])
                        P.op("pe", lambda e: e.matmul(S[:, cs], lhsT=ident_b[:], rhs=nat[:, (base + i) * NA_H + h, :],
                                                      start=False, stop=True), reads=[ident_b, nat], writes=[S])

                def fE(S=S, Pt=Pt, nt=nt):
                    n1 = min(nt, 4) * 128
                    P.op("act", lambda e: e.activation(out=Pt[:, 0:n1], in_=S[:, 0:n1], func=AF.Exp), reads=[S], writes=[Pt])
                    if nt > 4:
                        P.op("act", lambda e: e.activation(out=Pt[:, 512:nt * 128], in_=S[:, 512:nt * 128], func=AF.Exp),
                             reads=[S], writes=[Pt])

                def fV(Pt=Pt, O=O, V=V, offs=offs, rp=rp, nt=nt):
                    for i, o in enumerate(offs):
                        kp = rp + o + 2
                        P.op("pe", lambda e: e.matmul(O[:, 0:128], lhsT=V[:, kp, :], rhs=Pt[:, i * 128:(i + 1) * 128],
                                                      start=(i == 0), stop=(i == nt - 1)), reads=[V, Pt], writes=[O])
                    for i in range(nt):
                        P.op("pe", lambda e: e.matmul(O[:, 128:256], lhsT=ones_b[:], rhs=Pt[:, i * 128:(i + 1) * 128],
                                                      start=(i == 0), stop=(i == nt - 1)), reads=[ones_b, Pt], writes=[O])

                def fN(O=O, R=R, h=h, qs=qs):
                    P.op("dve", lambda e: e.reciprocal(out=R[:], in_=O[:, 128:256]), reads=[O], writes=[R])
                    P.op("dve", lambda e: e.tensor_tensor(out=OT[:, h, qs], in0=O[:, 0:128], in1=R[:], op=ALU.mult),
                         reads=[O, R], writes=[OT])
                it.S, it.E, it.V, it.N = fS, fE, fV, fN
                items.append(it)
            groups.append((load, items))
        for j in (range(4) if cfg.do_dil else ()):
            for g, d in enumerate(DIL):
                hd = 4 * g + j
                Lq = TOK // d
                Lk = Lq + 128
                nb = Lq // 128
                nm = nb + 1
                gi = len(groups)
                Q, Kt, V = qT[gi % 2], kT[gi % 2], vt[gi % 2]

                def load(g=g, j=j, d=d, Lk=Lk, nm=nm, Q=Q, Kt=Kt, V=V):
                    P.dma("sp", out=Q[:], in_=a["QB"][g][j], sb=Q, reads=[a["QBb"][g]], writes=[Q])
                    P.dma("sp", out=Kt[:, 0:d * Lk], in_=a["KB"][g][j].rearrange("p r l -> p (r l)"), sb=Kt,
                          reads=[a["KBb"][g]], writes=[Kt])
                    vbase = 192 if g == 0 else 0
                    for r in range(d):
                        src = a["VB"][g][vbase + r:vbase + r + (nm * 128 - 1) * d + 1:d, j * 128:(j + 1) * 128]
                        P.dma("sp", out=V[:, r * nm:(r + 1) * nm, :], in_=src.rearrange("(m p) c -> p m c", p=128),
                              sb=V, reads=[a["VBb"][g]], writes=[V])
                Ua = Uacc[:].rearrange("p (l r) -> p r l", r=d)
                Da = Dacc[:].rearrange("p (l r) -> p r l", r=d)
                items = []
                for r in range(d):
                    for n in range(nb):
                        k = nextk()
                        S, Pt, O = SA[k % 2], PT[k % 3], OB[k % 3]
                        qs = slice(r * Lq + n * 128, r * Lq + (n + 1) * 128)
                        it = Item()

                        def fS(S=S, Q=Q, Kt=Kt, r=r, n=n, Lk=Lk, qs=qs, hd=hd):
                            for t in range(2):
                                m = n + t
                                cs = slice(t * 128, (t + 1) * 128)
                                P.op("pe", lambda e: e.matmul(S[:, cs], lhsT=Kt[:, r * Lk + m * 128:r * Lk + (m + 1) * 128],
                                                              rhs=Q[:, qs], start=True, stop=False), reads=[Kt, Q], writes=[S])
                                P.op("pe", lambda e: e.matmul(S[:, cs], lhsT=ident_b[:],
                                                              rhs=abt[:, hd, (1 - t) * 128:(2 - t) * 128],
                                                              start=False, stop=True), reads=[ident_b, abt], writes=[S])

                        def fE(S=S, Pt=Pt, n=n, nb=nb, g=g, r=r):
                            lo_edge = (n == 0)
                            hi_edge = (n == nb - 1)
                            if not (lo_edge or hi_edge):
                                P.op("act", lambda e: e.activation(out=Pt[:, 0:256], in_=S[:, 0:256], func=AF.Exp),
                                     reads=[S], writes=[Pt])
                                return
                            for t in range(2):
                                edge = (t == 0 and lo_edge) or (t == 1 and hi_edge)
                                cs = slice(t * 128, (t + 1) * 128)
                                if edge:
                                    col = KB_COL(g, t, r)
                                    P.op("act", lambda e: e.activation(out=Pt[:, cs], in_=S[:, cs], func=AF.Exp,
                                                                       bias=kbs[:, col:col + 1], scale=1.0),
                                         reads=[S, kbs], writes=[Pt])
                                else:
                                    P.op("act", lambda e: e.activation(out=Pt[:, cs], in_=S[:, cs], func=AF.Exp),
                                         reads=[S], writes=[Pt])

                        def fV(Pt=Pt, O=O, V=V, r=r, n=n, nm=nm):
                            for t in range(2):
                                m = n + t
                                P.op("pe", lambda e: e.matmul(O[:, 0:128], lhsT=V[:, r * nm + m, :],
                                                              rhs=Pt[:, t * 128:(t + 1) * 128], start=(t == 0), stop=(t == 1)),
                                     reads=[V, Pt], writes=[O])
                            for t in range(2):
                                P.op("pe", lambda e: e.matmul(O[:, 128:256], lhsT=ones_b[:], rhs=Pt[:, t * 128:(t + 1) * 128],
                                                              start=(t == 0), stop=(t == 1)), reads=[ones_b, Pt], writes=[O])

                        def fN(O=O, Ua=Ua, Da=Da, r=r, n=n, g=g):
                            ls = slice(n * 128, (n + 1) * 128)
                            if g == 0:
                                P.op("dve", lambda e: e.tensor_copy(out=Ua[:, r, ls], in_=O[:, 0:128]), reads=[O], writes=[Uacc])
                                P.op("dve", lambda e: e.tensor_copy(out=Da[:, r, ls], in_=O[:, 128:256]), reads=[O], writes=[Dacc])
                            else:
                                P.op("dve", lambda e: e.tensor_tensor(out=Ua[:, r, ls], in0=O[:, 0:128], in1=Ua[:, r, ls],
                                                                      op=ALU.add), reads=[O, Uacc], writes=[Uacc])
                                P.op("dve", lambda e: e.tensor_tensor(out=Da[:, r, ls], in0=O[:, 128:256], in1=Da[:, r, ls],
                                                                      op=ALU.add), reads=[O, Dacc], writes=[Dacc])
                        it.S, it.E, it.V, it.N = fS, fE, fV, fN
                        items.append(it)
                if g == 2:
                    def post(j=j):
                        P.op("dve", lambda e: e.reciprocal(out=Dacc[:], in_=Dacc[:]), reads=[Dacc], writes=[Dacc])
                        P.op("dve", lambda e: e.tensor_tensor(out=OT[:, 8 + j, :], in0=Uacc[:], in1=Dacc[:], op=ALU.mult),
                             reads=[Uacc, Dacc], writes=[OT])
                    items[-1].post = post
                groups.append((load, items))
        flat = [(gi, ii, it) for gi, (ld, its) in enumerate(groups) for ii, it in enumerate(its)]
        if groups:
            groups[0][0]()
        prev = None
        for (gi, ii, it) in flat:
            it.S()
            if prev is not None:
                prev.E(); prev.V(); prev.N()
                if prev.post:
                    prev.post()
            if ii == 0 and gi + 1 < len(groups):
                groups[gi + 1][0]()
            prev = it
        if prev is not None:
            prev.E(); prev.V(); prev.N()
            if prev.post:
                prev.post()
        P.barrier()


def phase3a(nc, P, cfg, sb, ps, a):
    OT = a["OT"]
    with ExitStack() as st:
        Wa = sb("Wa", [128, 8, D], BF16, st)
        Wb = sb("Wb", [128, 4, D], BF16, st)
        for i in range(2):
            P.dma("pool", out=Wa[:, i * 4:(i + 1) * 4, :],
                  in_=a["wpa"][i * 512:(i + 1) * 512, :].rearrange("(c p) n -> p c n", p=128), sb=Wa, writes=[Wa])
        P.dma("pool", out=Wb[:], in_=a["wpb"].rearrange("(c p) n -> p c n", p=128), sb=Wb, writes=[Wb])
        gat = [sb("gat%d" % i, [128, 512], BF16, st) for i in range(2)]
        gbt = [sb("gbt%d" % i, [128, 512], BF16, st) for i in range(2)]
        t1 = [sb("t1_%d" % i, [128, 512], F32, st) for i in range(2)]
        t2 = [sb("t2_%d" % i, [128, 512], F32, st) for i in range(2)]
        ms = [sb("ms%d" % i, [128, TOK], BF16, st) for i in range(2)]
        ya = [ps("ya%d" % i, [128, 512], F32, st) for i in range(2)]
        yb = [ps("yb%d" % i, [128, 512], F32, st) for i in range(2)]
        k = 0
        for c in range(NCH):
            M = ms[c % 2]
            for tb in range(4):
                ts_ = slice(tb * 512, (tb + 1) * 512)
                YA, YB, G1, G2, T1, T2 = ya[k % 2], yb[k % 2], gat[k % 2], gbt[k % 2], t1[k % 2], t2[k % 2]
                k += 1
                P.dma("sp", out=G1[:], in_=a["GA"][c][:, ts_], sb=G1, reads=[a["GAb"]], writes=[G1])
                P.dma("sp", out=G2[:], in_=a["GB"][c][:, ts_], sb=G2, reads=[a["GBb"]], writes=[G2])
                for h in range(8):
                    P.op("pe", lambda e: e.matmul(YA[:], lhsT=Wa[:, h, c * 128:(c + 1) * 128], rhs=OT[:, h, ts_],
                                                  start=(h == 0), stop=(h == 7)), reads=[Wa, OT], writes=[YA])
                for j in range(4):
                    P.op("pe", lambda e: e.matmul(YB[:], lhsT=Wb[:, j, c * 128:(c + 1) * 128], rhs=OT[:, 8 + j, ts_],
                                                  start=(j == 0), stop=(j == 3)), reads=[Wb, OT], writes=[YB])
                P.op("dve", lambda e: e.tensor_tensor(out=T1[:], in0=YA[:], in1=G1[:], op=ALU.mult), reads=[YA, G1], writes=[T1])
                P.op("dve", lambda e: e.tensor_tensor(out=T2[:], in0=YB[:], in1=G2[:], op=ALU.mult), reads=[YB, G2], writes=[T2])
                P.op("pool", lambda e: e.tensor_tensor(out=M[:, ts_], in0=T1[:], in1=T2[:], op=ALU.add), reads=[T1, T2], writes=[M])
            P.dma("sp", out=a["MT"][c], in_=M[:], sb=M, reads=[M], writes=[a["MTb"]])


def phase3b(nc, P, cfg, sb, ps, a):
    dsti, gts = a["dsti"], a["gts"]
    ident_f, ones_b = a["ident_f"], a["ones_b"]
    with ExitStack() as st:
        Wo = sb("Wo", [128, NCH, D], BF16, st)
        for i in range(4):
            P.dma("pool", out=Wo[:, i * 4:(i + 1) * 4, :],
                  in_=a["wo"][i * 512:(i + 1) * 512, :].rearrange("(c p) n -> p c n", p=128), sb=Wo, writes=[Wo])
        rv = a["rowv"]
        A0 = sb("A0", [128, D], F32, st)
        B0 = sb("B0", [128, D], F32, st)
        G1v = sb("G1v", [128, D], F32, st)
        B1v = sb("B1v", [128, D], F32, st)
        tmpv = sb("tmpv", [128, D], F32, st)
        P.dma("sp", out=A0[:], in_=rv[:, 0:D].partition_broadcast(128), sb=A0, writes=[A0])
        P.dma("sp", out=B0[:], in_=rv[:, D:2 * D].partition_broadcast(128), sb=B0, writes=[B0])
        P.dma("sp", out=tmpv[:], in_=rv[:, 2 * D:3 * D].partition_broadcast(128), sb=tmpv, writes=[tmpv])
        P.dma("sp", out=G1v[:], in_=rv[:, 3 * D:4 * D].partition_broadcast(128), sb=G1v, writes=[G1v])
        P.dma("sp", out=B1v[:], in_=rv[:, 4 * D:5 * D].partition_broadcast(128), sb=B1v, writes=[B1v])
        P.op("dve", lambda e: e.tensor_scalar(out=A0[:], in0=A0[:], scalar1=ALPHA, scalar2=None, op0=ALU.mult),
             reads=[A0], writes=[A0])
        P.op("dve", lambda e: e.scalar_tensor_tensor(out=B0[:], in0=B0[:], scalar=ALPHA, in1=tmpv[:], op0=ALU.mult,
                                                     op1=ALU.add), reads=[B0, tmpv], writes=[B0])
        wrs = sb("wrs", [128, NCH, 72], F32, st)
        P.dma("sp", out=wrs[:], in_=a["wr"].rearrange("(c p) n -> p c n", p=128), sb=wrs, writes=[wrs])
        rr = sb("rr", [128, 136], F32, st)
        P.dma("sp", out=rr[:], in_=a["rowr"].partition_broadcast(128), sb=rr, writes=[rr])
        us = sb("us", [128, 128], BF16, st)
        P.dma("pool", out=us[:], in_=a["ustr"], sb=us, writes=[us])
        cntv = sb("cntv", [128, 64], F32, st)
        P.op("dve", lambda e: e.memset(cntv[:], 0.0), writes=[cntv])
        epsc = sb("epsc2", [128, 1], F32, st)
        P.op("dve", lambda e: e.memset(epsc[:], LN_EPS), writes=[epsc])
        zt = a["zt"]
        P.op("pool", lambda e: e.memset(zt[:, 0:8], 0.0), writes=[zt])

        mt = [sb("mt%d" % i, [128, NCH, 128], BF16, st) for i in range(2)]
        xt = [sb("x3_%d" % i, [128, D], F32, st) for i in range(2)]
        xn = sb("xn3", [128, D], F32, st)
        rt = sb("rt3", [128, D], F32, st)
        h1 = [sb("h1_%d" % i, [128, D], F32, st) for i in range(2)]
        h1b = [sb("h1b%d" % i, [128, D], BF16, st) for i in range(3)]
        h1T = sb("h1T", [128, NCH, 128], F32, st)
        stt = sb("stt3", [128, 4, 6], F32, st)
        mv = sb("mv3", [128, 8], F32, st)
        sm2 = [sb("sm3_%d" % i, [128, 512], F32, st) for i in range(2)]
        eb2 = [sb("eb3_%d" % i, [128, 64], BF16, st) for i in range(2)]
        mix = [ps("mix%d" % i, [128, 512], F32, st) for i in range(4)]
        tpf = [ps("tpf%d" % i, [128, 4, 128], F32, st) for i in range(2)]
        lgp = ps("lgp", [128, 128], F32, st)
        rkp = ps("rkp", [128, 128], F32, st)
        ntile = cfg.ntile3 if hasattr(cfg, "ntile3") else NTILE

        def ln_stats(X, col):
            for c in range(4):
                P.op("dve", lambda e, c=c: e.bn_stats(out=stt[:, c, :], in_=X[:, c * 512:(c + 1) * 512]), reads=[X], writes=[stt])
            P.op("dve", lambda e: e.bn_aggr(out=mv[:, col:col + 2], in_=stt[:]), reads=[stt], writes=[mv])
            P.op("act", lambda e: e.activation(out=mv[:, col + 3:col + 4], in_=mv[:, col + 1:col + 2], func=AF.Sqrt,
                                               bias=epsc[:, 0:1], scale=1.0), reads=[mv, epsc], writes=[mv])
            P.op("dve", lambda e: e.reciprocal(out=mv[:, col + 1:col + 2], in_=mv[:, col + 3:col + 4]), reads=[mv], writes=[mv])
            P.op("dve", lambda e: e.scalar_tensor_tensor(out=mv[:, col + 2:col + 3], in0=mv[:, col:col + 1], scalar=-1.0,
                                                         in1=mv[:, col + 1:col + 2], op0=ALU.mult, op1=ALU.mult),
                 reads=[mv], writes=[mv])

        def tile_vars(tt):
            return mt[tt % 2], xt[tt % 2], h1[tt % 2], h1b[tt % 3], slice(tt * 128, (tt + 1) * 128)

        def load(tt):
            M, X, H, HB, tsl = tile_vars(tt)
            P.dma("sp", out=M[:], in_=a["MT"][:, :, tsl].rearrange("c p t -> p c t"), sb=M, reads=[a["MTb"]], writes=[M])
            P.dma("sp", out=X[:], in_=a["xh"][HALO + tt * 128:HALO + (tt + 1) * 128, :], sb=X, writes=[X])

        def stageA(tt, part):
            M, X, H, HB, tsl = tile_vars(tt)
            if part == 'pe':
                for nb in range(4):
                    for c in range(NCH):
                        P.op("pe", lambda e: e.matmul(mix[nb][:], lhsT=M[:, c, :], rhs=Wo[:, c, nb * 512:(nb + 1) * 512],
                                                      start=(c == 0), stop=(c == NCH - 1)), reads=[M, Wo], writes=[mix[nb]])
                return
            if part == 'res':
                ln_stats(X, 0)
                P.op("act", lambda e: e.activation(out=xn[:], in_=X[:], func=AF.Identity, scale=mv[:, 1:2], bias=mv[:, 2:3]),
                     reads=[X, mv], writes=[xn])
                P.op("pool", lambda e: e.tensor_tensor(out=xn[:], in0=xn[:], in1=A0[:], op=ALU.mult), reads=[xn, A0], writes=[xn])
                P.op("pool", lambda e: e.tensor_tensor(out=xn[:], in0=xn[:], in1=B0[:], op=ALU.add), reads=[xn, B0], writes=[xn])
                for nb in range(4):
                    cs = slice(nb * 512, (nb + 1) * 512)
                    P.op("dve", lambda e: e.tensor_tensor(out=rt[:, cs], in0=mix[nb][:], in1=xn[:, cs], op=ALU.add),
                         reads=[mix[nb], xn], writes=[rt])
                return
            if part == 'ln1':
                ln_stats(rt, 4)
                P.op("act", lambda e: e.activation(out=H[:], in_=rt[:], func=AF.Identity, scale=mv[:, 5:6], bias=mv[:, 6:7]),
                     reads=[rt, mv], writes=[H])
                P.op("pool", lambda e: e.tensor_tensor(out=H[:], in0=H[:], in1=G1v[:], op=ALU.mult), reads=[H, G1v], writes=[H])
                return
            P.op("dve", lambda e: e.tensor_tensor(out=H[:], in0=H[:], in1=B1v[:], op=ALU.add), reads=[H, B1v], writes=[H])
            P.dma("sp", out=a["H1"][tsl, :], in_=H[:], sb=H, reads=[H], writes=[a["H1b"]])
            P.op("act", lambda e: e.copy(out=HB[:], in_=H[:]), reads=[H], writes=[HB])

        def stageB(tt, part):
            M, X, H, HB, tsl = tile_vars(tt)
            sm = sm2[tt % 2]
            eb = eb2[tt % 2]
            if part == '1pe':
                for q4 in range(4):
                    TP = tpf[q4 % 2]
                    for j in range(4):
                        c = q4 * 4 + j
                        P.op("pe", lambda e: e.transpose(out=TP[:, j, :], in_=H[:, c * 128:(c + 1) * 128], identity=ident_f[:]),
                             reads=[H, ident_f], writes=[TP])
                    P.op("act", lambda e: e.copy(out=h1T[:, q4 * 4:(q4 + 1) * 4, :], in_=TP[:]), reads=[TP], writes=[h1T], nw=True)
                for c in range(NCH):
                    P.op("pe", lambda e: e.matmul(lgp[:, 0:72], lhsT=h1T[:, c, :], rhs=wrs[:, c, :], start=(c == 0),
                                                  stop=(c == NCH - 1)), reads=[h1T, wrs], writes=[lgp])
            V = lambda fn, r, w: P.op("dve", fn, reads=r, writes=w)
            Lg = sm[:, 0:72]
            if part == '1dve':
                V(lambda e: e.tensor_tensor(out=Lg, in0=lgp[:, 0:72], in1=rr[:, 0:72], op=ALU.add), [lgp, rr], [sm])
            gl = sm[:, 0:8]
            el3 = sm[:, 8:72].rearrange("p (g j) -> p g j", g=8)
            gmax, ngmax, gsum, gprob = sm[:, 80:81], sm[:, 81:82], sm[:, 82:83], sm[:, 83:84]
            m1, m2, dlt, e2, den, p1, p2 = (sm[:, 84 + i:85 + i] for i in range(7))
            goh = sm[:, 96:104]
            ge = sm[:, 104:112]
            esel = sm[:, 112:120]
            oh1 = sm[:, 120:128]
            oh2 = sm[:, 128:136]
            msk = sm[:, 136:144]
            tmp3 = sm[:, 144:208].rearrange("p (g j) -> p g j", g=8)
            E1 = sm[:, 208:272]
            E2 = sm[:, 272:336]
            Es = sm[:, 336:400]
            slotf = sm[:, 400:464]
            d12 = sm[:, 464:466]
            if part == '1dve':
                V(lambda e: e.reduce_max(out=gmax, in_=gl, axis=AX.X), [sm], [sm])
            if part == '1dve':
                V(lambda e: e.tensor_scalar(out=goh, in0=gl, scalar1=gmax, scalar2=None, op0=ALU.is_equal), [sm], [sm])
            if part == '1dve':
                V(lambda e: e.tensor_scalar(out=ngmax, in0=gmax, scalar1=-1.0, scalar2=None, op0=ALU.mult), [sm], [sm])
            if part == '1dve':
                P.op("act", lambda e: e.activation(out=ge, in_=gl, func=AF.Exp, bias=ngmax, scale=1.0), reads=[sm], writes=[sm])
            if part == '1dve':
                V(lambda e: e.reduce_sum(out=gsum, in_=ge, axis=AX.X), [sm], [sm])
            if part == '1dve':
                V(lambda e: e.reciprocal(out=gprob, in_=gsum), [sm], [sm])
            if part == '1dve':
                V(lambda e: e.tensor_tensor(out=tmp3, in0=el3, in1=goh.unsqueeze(2).to_broadcast([128, 8, 8]), op=ALU.mult), [sm], [sm])
            if part == '1dve':
                V(lambda e: e.reduce_sum(out=esel, in_=tmp3.rearrange("p g j -> p j g"), axis=AX.X), [sm], [sm])
            if part == '1dve':
                V(lambda e: e.reduce_max(out=m1, in_=esel, axis=AX.X), [sm], [sm])
            if part == '1dve':
                V(lambda e: e.tensor_scalar(out=oh1, in0=esel, scalar1=m1, scalar2=None, op0=ALU.is_equal), [sm], [sm])
            if part == '1dve':
                V(lambda e: e.scalar_tensor_tensor(out=msk, in0=oh1, scalar=-1e30, in1=esel, op0=ALU.mult, op1=ALU.add), [sm], [sm])
            if part == '1dve':
                V(lambda e: e.reduce_max(out=m2, in_=msk, axis=AX.X), [sm], [sm])
            if part == '1dve':
                V(lambda e: e.tensor_scalar(out=oh2, in0=msk, scalar1=m2, scalar2=None, op0=ALU.is_equal), [sm], [sm])
            if part == '1dve':
                V(lambda e: e.tensor_tensor(out=dlt, in0=m2, in1=m1, op=ALU.subtract), [sm], [sm])
            if part == '1dve':
                P.op("act", lambda e: e.activation(out=e2, in_=dlt, func=AF.Exp), reads=[sm], writes=[sm])
            if part == '1dve':
                V(lambda e: e.tensor_scalar(out=den, in0=e2, scalar1=1.0, scalar2=None, op0=ALU.add), [sm], [sm])
            if part == '1dve':
                V(lambda e: e.reciprocal(out=p1, in_=den), [sm], [sm])
            if part == '1dve':
                V(lambda e: e.tensor_tensor(out=p2, in0=e2, in1=p1, op=ALU.mult), [sm], [sm])
            if part == '1dve':
                V(lambda e: e.tensor_tensor(out=gts[:, 2 * tt:2 * tt + 1], in0=gprob, in1=p1, op=ALU.mult), [sm], [gts])
            if part == '1dve':
                V(lambda e: e.tensor_tensor(out=gts[:, 2 * tt + 1:2 * tt + 2], in0=gprob, in1=p2, op=ALU.mult), [sm], [gts])
            gb3 = goh.unsqueeze(2).to_broadcast([128, 8, 8])
            if part == '1dve':
                V(lambda e: e.tensor_tensor(out=E1.rearrange("p (g j) -> p g j", g=8), in0=gb3,
                                            in1=oh1.unsqueeze(1).to_broadcast([128, 8, 8]), op=ALU.mult), [sm], [sm])
            if part == '1dve':
                V(lambda e: e.tensor_tensor(out=E2.rearrange("p (g j) -> p g j", g=8), in0=gb3,
                                            in1=oh2.unsqueeze(1).to_broadcast([128, 8, 8]), op=ALU.mult), [sm], [sm])
            if part == '1dve':
                V(lambda e: e.tensor_tensor(out=Es, in0=E1, in1=E2, op=ALU.add), [sm], [sm])
            if part == '1dve':
                V(lambda e: e.tensor_copy(out=eb[:], in_=Es), [sm], [eb])
            if part == '2pe':
                P.op("pe", lambda e: e.matmul(rkp[:, 0:64], lhsT=us[:], rhs=eb[:], start=True, stop=True), reads=[us, eb], writes=[rkp])
                P.op("pe", lambda e: e.matmul(rkp[:, 64:128], lhsT=ones_b[:], rhs=eb[:], start=True, stop=True), reads=[ones_b, eb], writes=[rkp])
            if part == '2dve':
                V(lambda e: e.tensor_tensor(out=slotf, in0=rkp[:, 0:64], in1=cntv[:], op=ALU.add), [rkp, cntv], [sm])
                V(lambda e: e.tensor_tensor(out=cntv[:], in0=rkp[:, 64:128], in1=cntv[:], op=ALU.add), [rkp, cntv], [cntv])
                V(lambda e: e.tensor_scalar(out=slotf, in0=slotf, scalar1=float(CAP - 1), scalar2=None, op0=ALU.min), [sm], [sm])
                V(lambda e: e.tensor_tensor(out=slotf, in0=slotf, in1=rr[:, 72:136], op=ALU.add), [sm, rr], [sm])
                V(lambda e: e.tensor_tensor(out=E1, in0=E1, in1=slotf, op=ALU.mult), [sm], [sm])
                V(lambda e: e.tensor_tensor(out=E2, in0=E2, in1=slotf, op=ALU.mult), [sm], [sm])
                V(lambda e: e.reduce_sum(out=d12[:, 0:1], in_=E1, axis=AX.X), [sm], [sm])
                V(lambda e: e.reduce_sum(out=d12[:, 1:2], in_=E2, axis=AX.X), [sm], [sm])
                V(lambda e: e.tensor_copy(out=dsti[:, 2 * tt:2 * tt + 2], in_=d12), [sm], [dsti])
                if "RT" in cfg.debug:
                    V(lambda e: e.tensor_copy(out=sm[:, 480:482], in_=d12), [sm], [sm])
                    V(lambda e: e.tensor_copy(out=sm[:, 482:484], in_=gts[:, 2 * tt:2 * tt + 2]), [sm, gts], [sm])
                    P.dma("sp", out=a["RT"][tsl, 0:4], in_=sm[:, 480:484], sb=sm, reads=[sm], writes=[a["RTb"]])
                for kk in range(2):
                    P.dma("pool", out=a["Xs"][:, :], in_=HB[:, :], sb=HB, reads=[HB, dsti, zt], writes=[a["Xsb"]],
                          indirect=dict(out_offset=bass.IndirectOffsetOnAxis(ap=dsti[:, 2 * tt + kk:2 * tt + kk + 1], axis=0),
                                        in_offset=None))


        load(0)
        if ntile > 1:
            load(1)
        stageA(0, 'pe')
        for i in range(ntile + 2):
            if i < ntile:
                stageA(i, 'res')
            if 1 <= i <= ntile:
                stageB(i - 1, '1pe')
            if 2 <= i:
                stageB(i - 2, '2pe')
            if i + 1 < ntile:
                stageA(i + 1, 'pe')
            if i + 2 < ntile:
                load(i + 2)
            if i < ntile:
                stageA(i, 'ln1')
            if 1 <= i <= ntile:
                stageB(i - 1, '1dve')
            if i < ntile:
                stageA(i, 'ln2')
            if 2 <= i:
                stageB(i - 2, '2dve')


def phase4(nc, P, cfg, sb, ps, a):
    ident_b = a["ident_b"]
    with ExitStack() as st:
        Wg = [sb("Wg%d" % i, [128, NCH, DE], BF16, st) for i in range(2)]
        Wu = [sb("Wu%d" % i, [128, NCH, DE], BF16, st) for i in range(2)]
        Wd = [sb("Wd%d" % i, [128, 4, D], BF16, st) for i in range(2)]
        Xe = [sb("Xe%d" % i, [128, D], BF16, st) for i in range(2)]
        XT = [sb("XT%d" % i, [128, NCH, 128], BF16, st) for i in range(2)]
        sg = sb("sg", [128, DE], F32, st)
        hid = sb("hid", [128, DE], BF16, st)
        hT = sb("hidT", [128, 4, 128], BF16, st)
        Yt = [sb("Yt%d" % i, [128, D], F32, st) for i in range(2)]
        tpx = [ps("tpx%d" % i, [128, 8, 128], BF16, st) for i in range(2)]
        gp = ps("gp", [128, DE], F32, st)
        up = ps("up", [128, DE], F32, st)
        tph = ps("tph", [128, 4, 128], BF16, st)
        dp = [ps("dp%d" % i, [128, 512], F32, st) for i in range(2)]
        kd = 0
        for e_ in range(cfg.nexp_run if hasattr(cfg, "nexp_run") else NEXP):
            i = e_ % 2
            P.dma("pool", out=Wg[i][:], in_=a["w_gate"][e_].rearrange("(c p) n -> p c n", p=128), sb=Wg[i], writes=[Wg[i]])
            P.dma("pool", out=Wu[i][:], in_=a["w_up"][e_].rearrange("(c p) n -> p c n", p=128), sb=Wu[i], writes=[Wu[i]])
            P.dma("pool", out=Wd[i][:], in_=a["w_down"][e_].rearrange("(c p) n -> p c n", p=128), sb=Wd[i], writes=[Wd[i]])
            X, XTt, Y = Xe[i], XT[i], Yt[i]
            P.dma("sp", out=X[:], in_=a["Xs"][e_ * CAP:(e_ + 1) * CAP, :], sb=X, reads=[a["Xsb"]], writes=[X])
            for half in range(2):
                TP = tpx[half]
                for j in range(8):
                    c = half * 8 + j
                    P.op("pe", lambda e: e.transpose(out=TP[:, j, :], in_=X[:, c * 128:(c + 1) * 128], identity=ident_b[:]),
                         reads=[X, ident_b], writes=[TP])
                if half == 0:
                    P.op("act", lambda e: e.copy(out=XTt[:, 0:8, :], in_=TP[:]), reads=[TP], writes=[XTt])
                else:
                    P.op("dve", lambda e: e.tensor_copy(out=XTt[:, 8:16, :], in_=TP[:]), reads=[TP], writes=[XTt])
            for c in range(NCH):
                P.op("pe", lambda e: e.matmul(gp[:], lhsT=XTt[:, c, :], rhs=Wg[i][:, c, :], start=(c == 0), stop=(c == NCH - 1)),
                     reads=[XTt, Wg[i]], writes=[gp])
            for c in range(NCH):
                P.op("pe", lambda e: e.matmul(up[:], lhsT=XTt[:, c, :], rhs=Wu[i][:, c, :], start=(c == 0), stop=(c == NCH - 1)),
                     reads=[XTt, Wu[i]], writes=[up])
            P.op("act", lambda e: e.activation(out=sg[:], in_=gp[:], func=AF.Silu), reads=[gp], writes=[sg])
            P.op("dve", lambda e: e.tensor_tensor(out=hid[:], in0=up[:], in1=sg[:], op=ALU.mult), reads=[up, sg], writes=[hid])
            for c in range(4):
                P.op("pe", lambda e: e.transpose(out=tph[:, c, :], in_=hid[:, c * 128:(c + 1) * 128], identity=ident_b[:]),
                     reads=[hid, ident_b], writes=[tph])
            P.op("act", lambda e: e.copy(out=hT[:], in_=tph[:]), reads=[tph], writes=[hT])
            for nb in range(4):
                DP = dp[kd % 2]
                kd += 1
                for c in range(4):
                    P.op("pe", lambda e: e.matmul(DP[:], lhsT=hT[:, c, :], rhs=Wd[i][:, c, nb * 512:(nb + 1) * 512],
                                                  start=(c == 0), stop=(c == 3)), reads=[hT, Wd[i]], writes=[DP])
                if nb % 2 == 0:
                    P.op("dve", lambda e: e.tensor_copy(out=Y[:, nb * 512:(nb + 1) * 512], in_=DP[:]), reads=[DP], writes=[Y])
                else:
                    P.op("act", lambda e: e.copy(out=Y[:, nb * 512:(nb + 1) * 512], in_=DP[:]), reads=[DP], writes=[Y])
            P.dma("sp", out=a["Ys"][e_ * CAP:(e_ + 1) * CAP, :], in_=Y[:], sb=Y, reads=[Y], writes=[a["Ysb"]])


def phase5(nc, P, cfg, sb, ps, a):
    dsti, gts = a["dsti"], a["gts"]
    with ExitStack() as st:
        G2v = sb("G2v", [128, D], F32, st)
        B2v = sb("B2v", [128, D], F32, st)
        rv = a["rowv"]
        P.dma("sp", out=G2v[:], in_=rv[:, 5 * D:6 * D].partition_broadcast(128), sb=G2v, writes=[G2v])
        P.dma("sp", out=B2v[:], in_=rv[:, 6 * D:7 * D].partition_broadcast(128), sb=B2v, writes=[B2v])
        Y1 = [sb("Y1_%d" % i, [128, D], F32, st) for i in range(3)]
        Y2 = [sb("Y2_%d" % i, [128, D], F32, st) for i in range(3)]
        Hh = [sb("Hh%d" % i, [128, D], F32, st) for i in range(3)]
        Oo = [sb("Oo%d" % i, [128, D], F32, st) for i in range(2)]
        stt = [sb("stt5_%d" % i, [128, 4, 6], F32, st) for i in range(2)]
        mvv = [sb("mv5_%d" % i, [128, 4], F32, st) for i in range(2)]
        epsc = sb("epsc5", [128, 1], F32, st)
        P.op("dve", lambda e: e.memset(epsc[:], LN_EPS), writes=[epsc])
        ntile = cfg.ntile3 if hasattr(cfg, "ntile3") else NTILE

        def load(tt):
            A, B, H = Y1[tt % 3], Y2[tt % 3], Hh[tt % 3]
            tsl = slice(tt * 128, (tt + 1) * 128)
            for kk, Yk in ((0, A), (1, B)):
                P.dma("pool", out=Yk[:, :], in_=a["Ys"][:, :], sb=Yk, reads=[a["Ysb"], dsti], writes=[Yk],
                      indirect=dict(out_offset=None,
                                    in_offset=bass.IndirectOffsetOnAxis(ap=dsti[:, 2 * tt + kk:2 * tt + kk + 1], axis=0)))
            P.dma("sp", out=H[:], in_=a["H1"][tsl, :], sb=H, reads=[a["H1b"]], writes=[H])

        def c1(tt):
            A, B, H = Y1[tt % 3], Y2[tt % 3], Hh[tt % 3]
            ST, mv = stt[tt % 2], mvv[tt % 2]
            P.op("act", lambda e: e.activation(out=A[:], in_=A[:], func=AF.Copy, scale=gts[:, 2 * tt:2 * tt + 1]),
                 reads=[A, gts], writes=[A])
            P.op("dve", lambda e: e.scalar_tensor_tensor(out=B[:], in0=B[:], scalar=gts[:, 2 * tt + 1:2 * tt + 2], in1=A[:],
                                                         op0=ALU.mult, op1=ALU.add), reads=[B, A, gts], writes=[B])
            P.op("dve", lambda e: e.scalar_tensor_tensor(out=H[:], in0=H[:], scalar=ALPHA, in1=B[:], op0=ALU.mult,
                                                         op1=ALU.add), reads=[H, B], writes=[H])
            for c in range(4):
                P.op("dve", lambda e, c=c: e.bn_stats(out=ST[:, c, :], in_=H[:, c * 512:(c + 1) * 512]), reads=[H], writes=[ST])
            P.op("dve", lambda e: e.bn_aggr(out=mv[:, 0:2], in_=ST[:]), reads=[ST], writes=[mv])
            P.op("act", lambda e: e.activation(out=mv[:, 3:4], in_=mv[:, 1:2], func=AF.Sqrt, bias=epsc[:, 0:1], scale=1.0),
                 reads=[mv, epsc], writes=[mv])
            P.op("dve", lambda e: e.reciprocal(out=mv[:, 1:2], in_=mv[:, 3:4]), reads=[mv], writes=[mv])
            P.op("dve", lambda e: e.scalar_tensor_tensor(out=mv[:, 2:3], in0=mv[:, 0:1], scalar=-1.0, in1=mv[:, 1:2],
                                                         op0=ALU.mult, op1=ALU.mult), reads=[mv], writes=[mv])

        def c2(tt):
            H, O, mv = Hh[tt % 3], Oo[tt % 2], mvv[tt % 2]
            tsl = slice(tt * 128, (tt + 1) * 128)
            P.op("act", lambda e: e.activation(out=O[:], in_=H[:], func=AF.Identity, scale=mv[:, 1:2], bias=mv[:, 2:3]),
                 reads=[H, mv], writes=[O])
            P.op("pool", lambda e: e.tensor_tensor(out=O[:], in0=O[:], in1=G2v[:], op=ALU.mult), reads=[O, G2v], writes=[O])
            P.op("dve", lambda e: e.tensor_tensor(out=O[:], in0=O[:], in1=B2v[:], op=ALU.add), reads=[O, B2v], writes=[O])
            P.dma("sp", out=a["out"][tsl, :], in_=O[:], sb=O, reads=[O], writes=[a["outb"]])

        load(0)
        if ntile > 1:
            load(1)
        c1(0)
        for tt in range(ntile):
            if tt + 2 < ntile:
                load(tt + 2)
            if tt + 1 < ntile:
                c1(tt + 1)
            c2(tt)


def KB_COL(g, t, r):
    off = (0, 2, 10)[g]
    return off + t * DIL[g] + r


def alibi_slopes(n):
    return np.array([2.0 ** (-8.0 * (i + 1) / n) for i in range(n)], dtype=np.float32)


def host_tables(rpb, q):
    rpb = np.asarray(rpb, np.float32)
    kc = np.arange(64)[:, None]
    qc = np.arange(64)[None, :]
    qcs = np.clip(qc - 8, 0, 48)
    colvalid = (kc >= qcs) & (kc < qcs + 16)
    dc = np.clip(kc - qc + 15, 0, 30)
    tab = np.full((NTAB, NA_H, 128, 128), NEG, np.float32)
    for rp in (0, 1, 7, 14, 15):
        cls = na_class(rp)
        base, offs = NA_CLASSES[cls]
        for i, o in enumerate(offs):
            for aa in range(2):
                for bb in range(2):
                    r = q * 32 + 2 * rp + bb
                    R = q * 32 + 2 * (rp + o) + aa
                    st = min(max(r - 4, 0), 120)
                    if not (0 <= R <= 127 and st <= R <= st + 7):
                        continue
                    blk = np.where(colvalid[None], rpb[:, R - r + 7][:, dc], NEG)
                    tab[base + i, :, aa * 64:(aa + 1) * 64, bb * 64:(bb + 1) * 64] = blk
    sl = alibi_slopes(12)
    aidx = np.arange(128)[:, None]
    iidx = np.arange(256)[None, :]
    rel = aidx + 64 - iidx
    ab = np.empty((12, 128, 256), np.float32)
    for hd in range(12):
        d = DIL[hd // 4]
        ab[hd] = np.where(np.abs(rel) <= 64, -sl[hd] * (np.abs(rel) * d).astype(np.float32), NEG)
    kb = np.zeros((128, 96), np.float32)
    apos = np.arange(128)
    for g, d in enumerate(DIL):
        Lq = TOK // d
        for t in range(2):
            for r in range(d):
                m = 0 if t == 0 else Lq // 128
                tok = (m * 128 - 64 + apos) * d + r + q * TOK
                kb[:, KB_COL(g, t, r)] = np.where((tok >= 0) & (tok < 8192), 0.0, NEG)
    return tab.reshape(NTAB * NA_H, 128, 128), ab, kb


def f(a):
    return np.asarray(a, np.float32)


def host_inputs(inp, core):
    b, q = divmod(core, 4)
    x = np.asarray(inp["x"], np.float32)
    xhal = np.zeros((TOK + 2 * HALO, D), np.float32)
    lo = q * TOK - HALO
    hi = q * TOK + TOK + HALO
    a, z = max(lo, 0), min(hi, 8192)
    xhal[a - lo:z - lo] = x[b, a:z]
    cvec = np.zeros((128, 160), np.float32)
    cvec[:, 0:16] = np.asarray(inp["ln0_g"], np.float32).reshape(16, 128).T
    cvec[:, 16:32] = np.asarray(inp["ln0_b"], np.float32).reshape(16, 128).T
    b_in = np.asarray(inp["b_in"], np.float32)[0]
    cvec[:, 32:124] = b_in.reshape(92, 128).T
    bv = np.concatenate([b_in[2048:3072], b_in[6144:7680]])[None, :]
    tabs = host_tables(np.asarray(inp["rpb"])[0], q)
    return {
        "xh": xhal,
        "w_in": np.asarray(inp["w_in"], np.float32)[0],
        "cvec": cvec,
        "bv": np.ascontiguousarray(bv),
        "identf": np.eye(128, dtype=np.float32),
        "natab": tabs[0], "abt": tabs[1], "kbias": tabs[2],
        "wpa": f(inp["w_proj_a"])[0], "wpb": f(inp["w_proj_b"])[0], "wo": f(inp["w_o"])[0],
        "rowv": np.concatenate([f(inp["ln0_g"]), f(inp["ln0_b"]), f(inp["b_o"])[0], f(inp["ln1_g"])[0],
                                f(inp["ln1_b"])[0], f(inp["ln2_g"])[0], f(inp["ln2_b"])[0]])[None, :],
        "wr": np.concatenate([f(inp["w_router_group"])[0], f(inp["w_router_expert"])[0]], axis=1),
        "rowr": np.concatenate([f(inp["b_router_group"])[0], f(inp["b_router_expert"])[0],
                                (np.arange(64) * CAP).astype(np.float32)])[None, :],
        "ustr": np.triu(np.ones((128, 128), np.float32), 1),
        "w_gate": f(inp["w_gate"])[0], "w_up": f(inp["w_up"])[0], "w_down": f(inp["w_down"])[0],
    }


def kernel(**inp):
    cfg = Cfg()
    nc = build(cfg)
    in_maps = [host_inputs(inp, c) for c in range(NCORES)]
    res = run_bass_kernel_spmd(nc, in_maps, core_ids=list(range(NCORES)))
    out = np.zeros((2, 8192, D), np.float32)
    for c in range(NCORES):
        b, q = divmod(c, 4)
        out[b, q * TOK:(q + 1) * TOK] = res.results[c]["out"]
    return out
```
